# Optimizing a Trainium2 kernel written in Bass

```python
import math
import jax, jax.numpy as jnp
from jax import lax
import numpy as np

D_MODEL = 1024
BATCH = 32
SEQ = 2048
DEPTH = 4

CHUNK = 64
MIX_WIDTH = D_MODEL
SB_HEADS = 8
SB_HEAD_DIM = 64
SB_WIDTH = SB_HEADS * SB_HEAD_DIM
SB_BLOCK = 128
POOL_WINDOWS = (2, 4, 8, 16)
POOL_GROUPS = len(POOL_WINDOWS)
POOL_GROUP_DIM = 64
POOL_WIDTH = POOL_GROUPS * POOL_GROUP_DIM
CONV_WIDTH = MIX_WIDTH - SB_WIDTH - POOL_WIDTH
CONV_KERNEL = 31
IN_COLS = 3 * SB_WIDTH + POOL_WIDTH + 2 * CONV_WIDTH
FFN_HIDDEN = int(math.ceil((8 * D_MODEL / 3) / 256) * 256)
RMS_EPS = 1e-6
LN_EPS = 1e-5

kernel_name = "hybrid_sb_pool_conformer_trunk"


def _rmsnorm(x, g):
    xf = x.astype(jnp.float32)
    y = xf * lax.rsqrt(jnp.mean(xf * xf, axis=-1, keepdims=True) + RMS_EPS)
    return (y * g.astype(jnp.float32)).astype(x.dtype)


def _layernorm(x, g, b):
    xf = x.astype(jnp.float32)
    mu = jnp.mean(xf, axis=-1, keepdims=True)
    var = jnp.mean(jnp.square(xf - mu), axis=-1, keepdims=True)
    y = (xf - mu) * lax.rsqrt(var + LN_EPS)
    return (y * g.astype(jnp.float32) + b.astype(jnp.float32)).astype(x.dtype)


def _stick_breaking_attention(q, k, v):
    b, t, h, dh = q.shape
    scale = 1.0 / math.sqrt(dh)
    qh = jnp.transpose(q, (0, 2, 1, 3))
    kh = jnp.transpose(k, (0, 2, 1, 3))
    vh = jnp.transpose(v, (0, 2, 1, 3)).astype(jnp.float32)
    outs = []
    for i in range(t // SB_BLOCK):
        q0 = i * SB_BLOCK
        k_end = q0 + SB_BLOCK
        qb = qh[:, :, q0:k_end]
        kb = kh[:, :, :k_end]
        vb = vh[:, :, :k_end]
        z = jnp.einsum('bhqd,bhkd->bhqk', qb, kb).astype(jnp.float32) * scale
        t_idx = q0 + jnp.arange(SB_BLOCK)[:, None]
        s_idx = jnp.arange(k_end)[None, :]
        strict = s_idx < t_idx
        log_beta = jax.nn.log_sigmoid(z)
        log_1mb = jnp.where(strict, jax.nn.log_sigmoid(-z), 0.0)
        suffix = lax.cumsum(log_1mb, axis=3, reverse=True) - log_1mb
        w = jnp.where(strict, jnp.exp(log_beta + suffix), 0.0)
        outs.append(jnp.einsum('bhqk,bhkd->bhqd', w, vb))
    o = jnp.concatenate(outs, axis=2)
    return jnp.transpose(o, (0, 2, 1, 3)).reshape(b, t, h * dh)


def _multiscale_pool(u, pool_w, pool_scale):
    b, t, _ = u.shape
    uf = u.astype(jnp.float32).reshape(b, t, POOL_GROUPS, POOL_GROUP_DIM)
    cs = jnp.cumsum(uf, axis=1)
    pos = jnp.arange(t, dtype=jnp.float32)[None, :, None]
    pooled = []
    for g, w in enumerate(POOL_WINDOWS):
        csg = cs[:, :, g]
        shifted = jnp.pad(csg, ((0, 0), (w, 0), (0, 0)))[:, :t]
        count = jnp.minimum(pos + 1.0, float(w))
        pooled.append((csg - shifted) / count - uf[:, :, g])
    p = jnp.stack(pooled, axis=2)
    y = jnp.einsum('btgc,gcd->btgd', p, pool_w.astype(jnp.float32))
    y = y.reshape(b, t, POOL_WIDTH) * pool_scale.astype(jnp.float32)
    return y.astype(u.dtype)


def _conformer_conv(u, conv_w, conv_b, ln_g, ln_b, pw_out):
    a, gate = jnp.split(u, 2, axis=-1)
    h = a * jax.nn.sigmoid(gate)
    kern = conv_w[:, None, :].astype(h.dtype)
    h = lax.conv_general_dilated(
        h, kern, window_strides=(1,), padding=[(CONV_KERNEL - 1, 0)],
        dimension_numbers=('NWC', 'WIO', 'NWC'), feature_group_count=CONV_WIDTH)
    h = h + conv_b
    h = _layernorm(h, ln_g, ln_b)
    h = jax.nn.silu(h)
    return jnp.einsum('btc,cd->btd', h, pw_out)


def setup_inputs(seed: int = 0) -> dict:
    key = jax.random.key(seed)
    ks = jax.random.split(key, 20)
    f32 = jnp.float32

    def nrm(k, shape, scale):
        return jax.random.normal(k, shape, f32) * scale

    return {
        "x": nrm(ks[0], (BATCH, SEQ, D_MODEL), 1.0),
        "norm_mix_g": 1.0 + nrm(ks[1], (DEPTH, D_MODEL), 0.05),
        "w_in": nrm(ks[2], (DEPTH, D_MODEL, IN_COLS), D_MODEL ** -0.5),
        "sb_q_g": 1.0 + nrm(ks[3], (DEPTH, SB_HEAD_DIM), 0.05),
        "sb_k_g": 1.0 + nrm(ks[4], (DEPTH, SB_HEAD_DIM), 0.05),
        "pool_w": nrm(ks[5], (DEPTH, POOL_GROUPS, POOL_GROUP_DIM, POOL_GROUP_DIM), POOL_GROUP_DIM ** -0.5),
        "pool_scale": 1.0 + nrm(ks[6], (DEPTH, POOL_WIDTH), 0.1),
        "conv_w": nrm(ks[7], (DEPTH, CONV_KERNEL, CONV_WIDTH), CONV_KERNEL ** -0.5),
        "conv_b": nrm(ks[8], (DEPTH, CONV_WIDTH), 0.02),
        "conv_ln_g": 1.0 + nrm(ks[9], (DEPTH, CONV_WIDTH), 0.05),
        "conv_ln_b": nrm(ks[10], (DEPTH, CONV_WIDTH), 0.02),
        "conv_pw": nrm(ks[11], (DEPTH, CONV_WIDTH, CONV_WIDTH), CONV_WIDTH ** -0.5),
        "w_out": nrm(ks[12], (DEPTH, MIX_WIDTH, D_MODEL), MIX_WIDTH ** -0.5),
        "norm_ffn_g": 1.0 + nrm(ks[13], (DEPTH, D_MODEL), 0.05),
        "ffn_w_gu": nrm(ks[14], (DEPTH, D_MODEL, 2 * FFN_HIDDEN), D_MODEL ** -0.5),
        "ffn_w_down": nrm(ks[15], (DEPTH, FFN_HIDDEN, D_MODEL), FFN_HIDDEN ** -0.5),
    }


def reference(x, norm_mix_g, w_in, sb_q_g, sb_k_g, pool_w, pool_scale, conv_w, conv_b,
              conv_ln_g, conv_ln_b, conv_pw, w_out, norm_ffn_g, ffn_w_gu, ffn_w_down):
    b, t, _ = x.shape
    split_pts = [SB_WIDTH, 2 * SB_WIDTH, 3 * SB_WIDTH, 3 * SB_WIDTH + POOL_WIDTH]
    for l in range(DEPTH):
        h = _rmsnorm(x, norm_mix_g[l])
        proj = jnp.einsum('btd,de->bte', h, w_in[l])
        q, k, v, u_pool, u_conv = jnp.split(proj, split_pts, axis=-1)
        q = _rmsnorm(q.reshape(b, t, SB_HEADS, SB_HEAD_DIM).astype(jnp.float32), sb_q_g[l])
        k = _rmsnorm(k.reshape(b, t, SB_HEADS, SB_HEAD_DIM).astype(jnp.float32), sb_k_g[l])
        v = v.reshape(b, t, SB_HEADS, SB_HEAD_DIM)
        a_out = _stick_breaking_attention(q, k, v).astype(x.dtype)
        p_out = _multiscale_pool(u_pool, pool_w[l], pool_scale[l])
        c_out = _conformer_conv(u_conv, conv_w[l], conv_b[l], conv_ln_g[l],
                                conv_ln_b[l], conv_pw[l]).astype(x.dtype)
        mix = jnp.concatenate([a_out, p_out, c_out], axis=-1)
        x = x + jnp.einsum('btm,md->btd', mix, w_out[l])
        h2 = _rmsnorm(x, norm_ffn_g[l])
        gu = jnp.einsum('btd,df->btf', h2, ffn_w_gu[l])
        g, u = jnp.split(gu, 2, axis=-1)
        x = x + jnp.einsum('btf,fd->btd', jax.nn.silu(g) * u, ffn_w_down[l])
    return x
```

```python
import numpy as np
import concourse.bass as bass
import concourse.mybir as mybir
from concourse.bass_utils import run_bass_kernel_spmd

F32 = mybir.dt.float32
BF16 = mybir.dt.bfloat16
AF = mybir.ActivationFunctionType
ALU = mybir.AluOpType

D = 1024
T = 2048
L_FULL = 4
NSEQ_CORE = 4
NCORES = 8
TG = 1024
NTG = T // TG
CH = 512
HID = 2816
NF = HID // 128
IN_COLS = 2304
NWIN = IN_COLS // 256
CONVK = 31
FPASS = [(0, 8), (8, 15), (15, 22)]
C_GQ, C_GK, C_PS, C_CB, C_LG, C_LB, C_CW = 0, 1, 2, 4, 6, 8, 10
NCOL = 10 + 2 * CONVK
K_ID, K_TRI, K_ONE, K_NEG, K_END = 0, 128, 256, 384, 512
KF_B64, KF_O256, KF_FIX, KF_END = 0, 128, 256, 288
TMPW = 544
CONV_ENG = ['dve', 'dve']
NWARM = 20
NFILL = 2
SIDE_CONV = True
A_ENG = 'dve'
BLK_ORDER = [7, 8, 6, 0, 1, 2, 3, 4, 5]


class Prog:
    def __init__(self, nc):
        self.nc = nc
        self.E = {'pe': nc.tensor, 'act': nc.scalar, 'dve': nc.vector, 'pool': nc.gpsimd, 'sp': nc.sync}
        self.esem = {}
        self.ecnt = {}
        for e in ['pe', 'act', 'dve', 'pool']:
            self.esem[e] = nc.alloc_semaphore('sem_' + e)
            self.ecnt[e] = 0
        self.waited = {}
        self.res = {}
        self.dsem = {}
        self.nins = 0

    @staticmethod
    def _flat(keys):
        out = []
        for k in keys:
            if isinstance(k, list):
                out.extend(Prog._flat(k))
            else:
                out.append(k)
        return out

    def _deps(self, reads, writes):
        reads = self._flat(reads)
        writes = self._flat(writes)
        toks = []
        for k in reads:
            r = self.res.get(k)
            if r is not None and r[0] is not None:
                toks.append(r[0])
        for k in writes:
            r = self.res.get(k)
            if r is not None:
                if r[0] is not None:
                    toks.append(r[0])
                toks.extend(r[1].values())
        return toks

    def _commit(self, tok, reads, writes):
        reads = self._flat(reads)
        writes = self._flat(writes)
        for k in reads:
            r = self.res.get(k)
            if r is None:
                r = [None, {}]
                self.res[k] = r
            old = r[1].get(tok[0])
            if old is None or old[2] < tok[2]:
                r[1][tok[0]] = tok
        for k in writes:
            self.res[k] = [tok, {}]

    def _wait(self, eng, toks):
        best = {}
        for (name, sem, val) in toks:
            if eng == 'pe' and name == 'sem_pe':
                continue
            b = best.get(name)
            if b is None or b[1] < val:
                best[name] = (sem, val)
        for name, (sem, val) in best.items():
            if self.waited.get((eng, name), 0) >= val:
                continue
            self.E[eng].wait_ge(sem, val)
            self.waited[(eng, name)] = val
            self.nins += 1

    def op(self, eng, reads, writes, fn):
        self._wait(eng, self._deps(reads, writes))
        ins = fn(self.E[eng])
        self.ecnt[eng] += 1
        ins.then_inc(self.esem[eng], 1)
        tok = ('sem_' + eng, self.esem[eng], self.ecnt[eng])
        self._commit(tok, reads, writes)
        self.nins += 1
        return tok

    def mm(self, reads, writes, mms):
        self._wait('pe', self._deps(reads, writes))
        ins = None
        for m in mms:
            ins = self.nc.tensor.matmul(**m)
            self.nins += 1
        self.ecnt['pe'] += 1
        ins.then_inc(self.esem['pe'], 1)
        tok = ('sem_pe', self.esem['pe'], self.ecnt['pe'])
        self._commit(tok, reads, writes)
        return tok

    def mm_batch(self, groups):
        toks = []
        for (reads, writes, mms) in groups:
            toks += self._deps(reads, writes)
        self._wait('pe', toks)
        for (reads, writes, mms) in groups:
            ins = None
            for m in mms:
                ins = self.nc.tensor.matmul(**m)
                self.nins += 1
            self.ecnt['pe'] += 1
            ins.then_inc(self.esem['pe'], 1)
            tok = ('sem_pe', self.esem['pe'], self.ecnt['pe'])
            self._commit(tok, reads, writes)

    def pe_ops(self, reads, writes, fns):
        self._wait('pe', self._deps(reads, writes))
        ins = None
        for f in fns:
            ins = f(self.nc.tensor)
            self.nins += 1
        self.ecnt['pe'] += 1
        ins.then_inc(self.esem['pe'], 1)
        tok = ('sem_pe', self.esem['pe'], self.ecnt['pe'])
        self._commit(tok, reads, writes)
        return tok

    def dma(self, q, pairs, reads, writes, semkey):
        self._wait(q, self._deps(reads, writes))
        d = self.dsem.get(semkey)
        if d is None:
            d = [self.nc.alloc_semaphore('dma_' + semkey), 0]
            self.dsem[semkey] = d
        for (out, in_) in pairs:
            self.E[q].dma_start(out=out, in_=in_).then_inc(d[0], 16)
            d[1] += 16
            self.nins += 1
        tok = ('dma_' + semkey, d[0], d[1])
        self._commit(tok, reads, writes)
        return tok

    def final_wait(self, eng, keys):
        toks = []
        for k in self._flat(keys):
            r = self.res.get(k)
            if r is not None:
                if r[0] is not None:
                    toks.append(r[0])
                toks.extend(r[1].values())
        self._wait(eng, toks)


class WStream:
    def __init__(self, prog, name, nslots, shape, plan, queue):
        self.p = prog
        self.name = name
        self.n = nslots
        self.plan = plan
        self.queue = queue
        self.slots = [prog.nc.alloc_sbuf_tensor(f'{name}_s{i}', shape, BF16).ap() for i in range(nslots)]
        self.issued = 0
        self.taken = 0
        self.released = 0

    def key(self, i):
        return (self.name, i % self.n)

    def prefetch(self):
        while self.issued < len(self.plan) and self.issued < self.released + self.n:
            i = self.issued
            self.p.dma(self.queue, [(self.slots[i % self.n], self.plan[i])], [], [self.key(i)],
                       f'{self.name}{i % self.n}')
            self.issued += 1

    def take(self):
        i = self.taken
        assert i < self.issued, (self.name, i, self.issued)
        self.taken += 1
        return self.slots[i % self.n], self.key(i)

    def release(self, k=1):
        self.released += k
        self.prefetch()


def build_program(n_layers=L_FULL, n_seq=NSEQ_CORE, wq='pool'):
    nc = bass.Bass("TRN2", target_bir_lowering=False)
    P = Prog(nc)
    L = n_layers

    x_d = nc.dram_tensor("x", [n_seq, T, D], F32, kind="ExternalInput").ap()
    win_d = nc.dram_tensor("win", [L, NWIN, 128, 8, 256], F32, kind="ExternalInput").ap()
    wout_d = nc.dram_tensor("wout", [L, 8, 128, 1024], F32, kind="ExternalInput").ap()
    wgu_d = nc.dram_tensor("wgu", [L, NF, 128, 8, 256], F32, kind="ExternalInput").ap()
    wdn_d = nc.dram_tensor("wdn", [L, NF, 128, 1024], F32, kind="ExternalInput").ap()
    smw_d = nc.dram_tensor("smw", [L, 128, 768], F32, kind="ExternalInput").ap()
    cols_d = nc.dram_tensor("cols", [128, L, NCOL], F32, kind="ExternalInput").ap()
    gains_d = nc.dram_tensor("gains", [L, 2, D], F32, kind="ExternalInput").ap()
    cbf_d = nc.dram_tensor("cbf", [128, K_END], F32, kind="ExternalInput").ap()
    cf_d = nc.dram_tensor("cf", [128, KF_END], F32, kind="ExternalInput").ap()
    out_d = nc.dram_tensor("out", [n_seq, T, D], F32, kind="ExternalOutput").ap()

    def sb(name, shape, dt):
        return nc.alloc_sbuf_tensor(name, shape, dt).ap()

    X = sb("X", [128, 16, D], F32)
    hT = sb("hT", [128, 8, TG], BF16)
    kT = sb("kT", [128, 4, T], BF16)
    V = sb("V", [128, 16, 512], BF16)
    qT = sb("qT", [128, 4, TG], BF16)
    MX = sb("MX", [128, 8, TG], BF16)
    NTMP = 13
    NROT = 8
    tmpool = sb("tmpool", [128, NTMP, TMPW], F32)
    tmp = [tmpool[:, i, :] for i in range(NTMP)]
    up = sb("up", [128, 2, 16 + CH], F32)
    hc2 = sb("hc2", [128, 2, 2, 30 + CH], BF16)
    dgb = sb("dgb", [128, 4, 128], BF16)
    gain = tmpool[:, 9:11, 0:512]
    KGAIN = [('tmp', 9, 0), ('tmp', 9, 1), ('tmp', 10, 0), ('tmp', 10, 1)]
    cbf = sb("cbf_sb", [128, K_END], BF16)
    z512 = sb("z512", [128, CH], BF16)
    cf = sb("cf_sb", [128, KF_END], F32)
    cols = sb("cols_sb", [128, L, NCOL], F32)
    gq8 = sb("gq8", [128, L], F32)
    smw = sb("smw_sb", [128, 768], BF16)
    ss = sb("ss", [128, 8], F32)
    rs = sb("rs", [128, 8], F32)
    rs2 = sb("rs2", [128, 8], F32)
    PS = nc.alloc_psum_tensor("PS", [128, 8, 512], F32).ap()

    ident = cbf[:, K_ID:K_ID + 128]
    triI = cbf[:, K_TRI:K_TRI + 128]
    ones = cbf[:, K_ONE:K_ONE + 128]
    negm = cbf[:, K_NEG:K_NEG + 128]
    b64 = cf[:, KF_B64:KF_B64 + 128]
    o256 = cf[:, KF_O256:KF_O256 + 128]

    def psk(b):
        return [('ps', b, 0), ('ps', b, 1)]

    tstate = {'i': 0}

    def talloc():
        i = tstate['i'] % NROT
        tstate['i'] += 1
        return tmp[i], [('tmp', i, 0), ('tmp', i, 1)]

    pstate = {'i': 0}

    def palloc(banks=(0, 1, 2, 3, 4, 5, 7)):
        b = banks[pstate['i'] % len(banks)]
        pstate['i'] += 1
        return b

    planA, planB = [], []
    for s in range(n_seq):
        for l in range(L):
            for tg in range(NTG):
                for j in BLK_ORDER:
                    planA.append(win_d[l, j])
                for c in range(8):
                    planB.append(wout_d[l, c])
                for (f0, f1) in FPASS:
                    for f in range(f0, f1):
                        planA.append(wgu_d[l, f])
                    for f in range(f0, f1):
                        planB.append(wdn_d[l, f])
    WA = WStream(P, 'wa', 3, [128, 8, 256], planA, wq)
    WB = WStream(P, 'wb', 8, [128, 1024], planB, wq)

    P.dma('pool', [(cbf, cbf_d)], [], ['cbf'], 'c0')
    P.dma('sp', [(cf, cf_d)], [], ['cf'], 'c1')
    P.dma('sp', [(cols, cols_d)], [], ['cols'], 'c2')
    P.op('pool', [], ['z512'], lambda e: e.memset(z512, 0.0))
    P.op('act', ['cols'], ['gq8'], lambda e: e.mul(out=gq8, in_=cols[:, :, C_GQ], mul=0.125))
    WA.prefetch()
    WB.prefetch()

    sidest = {'gen': None, 'left': 0, 'pe': 0, 'pending': [], 'look': None}

    def side_flush():
        for g in sidest['pending']:
            P.mm(*g)
        del sidest['pending'][:]

    def side_pull(k, defer=False):
        n = 0
        while n < k:
            it = sidest['look']
            sidest['look'] = None
            if it is None:
                g = sidest['gen']
                if g is None:
                    return
                try:
                    it = next(g)
                except StopIteration:
                    sidest['gen'] = None
                    return
            kind = it[0]
            if kind == 'free':
                it[1]()
            elif kind == 'pe':
                if defer:
                    if len(sidest['pending']) >= 2:
                        sidest['look'] = it
                        return
                    sidest['pending'].append(it[1])
                else:
                    P.mm(*it[1])
                n += 1
            else:
                if sidest['pending']:
                    sidest['look'] = it
                    return
                it[1]()
                n += 1

    def norm_phase(l, which, tiles):
        P.dma('sp', [(gain, gains_d[l, which].partition_broadcast(128).rearrange("p (a b) -> p a b", a=2))],
              [], [KGAIN], 'gain')
        for i, tl in enumerate(tiles):
            jk, jkk = talloc()
            P.op('act', [('X', tl)], [jkk, ('ss', i)],
                 lambda e: e.activation(out=jk[:, 0:512].bitcast(BF16), in_=X[:, tl, :], func=AF.Square,
                                        accum_out=ss[:, i:i + 1]))
        ssk = [('ss', i) for i in range(8)]
        P.op('act', ssk, ['rs'], lambda e: e.activation(out=rs, in_=ss, func=AF.Ln, scale=1.0 / D, bias=epsm6))
        P.op('act', ['rs'], ['rs2'], lambda e: e.activation(out=rs2, in_=rs, func=AF.Exp, scale=-0.5))
        for i, tl in enumerate(tiles):
            xt_, xnk = talloc()
            xb = xt_[:, 0:512].bitcast(BF16)
            P.op('dve', [('X', tl), 'rs2', KGAIN], [xnk],
                 lambda e: e.scalar_tensor_tensor(out=xb.rearrange("p (a b) -> p a b", a=2),
                                                  in0=X[:, tl, :].rearrange("p (a b) -> p a b", a=2),
                                                  scalar=rs2[:, i:i + 1], in1=gain,
                                                  op0=ALU.mult, op1=ALU.mult))
            b = palloc()
            pb = PS[:, b, :].bitcast(BF16)
            P.pe_ops([xnk, 'cbf'], psk(b),
                     [(lambda e, c=c: e.transpose(out=pb[:, c * 128:(c + 1) * 128],
                                                  in_=xb[:, c * 128:(c + 1) * 128], identity=ident))
                      for c in range(8)])
            P.op('act', psk(b), [('hT', i)],
                 lambda e: e.copy(out=hT[:, :, i * 128:(i + 1) * 128],
                                  in_=pb.rearrange("p (c t) -> p c t", c=8)))

    def fm_matmul(b, w, wkey, col0, tc):
        P.mm([wkey] + [('hT', tc * 4 + i) for i in range(4)], psk(b),
             [dict(out=PS[:, b, :], lhsT=w[:, c, col0:col0 + 128], rhs=hT[:, c, tc * CH:(tc + 1) * CH],
                   start=(c == 0), stop=(c == 7)) for c in range(8)])

    def qk_block(l, blk, w, wkey, tg):
        isq = blk < 2
        for f2 in range(2):
            fc = (blk % 2) * 2 + f2
            gcol = gq8[:, l:l + 1] if isq else cols[:, l, C_GK:C_GK + 1]
            for tc in range(2):
                b = palloc()
                fm_matmul(b, w, wkey, f2 * 128, tc)
                sq, sqk = talloc()
                P.op('act', psk(b), [sqk], lambda e: e.activation(out=sq[:, 0:CH], in_=PS[:, b, :], func=AF.Square))
                b2 = palloc()
                P.mm([sqk, 'cf'], psk(b2), [dict(out=PS[:, b2, :], lhsT=b64, rhs=sq[:, 0:CH], start=True, stop=True)])
                sd, sdk = talloc()
                P.op('act', psk(b2), [sdk],
                     lambda e: e.activation(out=sd[:, 0:CH], in_=PS[:, b2, :], func=AF.Ln, bias=epsm6))
                ri, rik = sd, sdk
                P.op('act', [sdk], [sdk],
                     lambda e: e.activation(out=sd[:, 0:CH], in_=sd[:, 0:CH], func=AF.Exp, scale=-0.5))
                if isq:
                    dst = qT[:, fc, tc * CH:(tc + 1) * CH]
                    dk = ('qT', fc, tc)
                else:
                    g0 = tg * TG + tc * CH
                    dst = kT[:, fc, g0:g0 + CH]
                    dk = ('kT', fc, tg * 2 + tc)
                P.op('dve', psk(b) + [rik, 'cols', 'gq8'], [dk],
                     lambda e: e.scalar_tensor_tensor(out=dst, in0=PS[:, b, :], scalar=gcol, in1=ri[:, 0:CH],
                                                      op0=ALU.mult, op1=ALU.mult))

    def v_block(blk, w, wkey, tg):
        vb = blk - 4
        for i in range(8):
            tl = tg * 8 + i
            b = palloc()
            P.mm([wkey, ('hT', i)], psk(b),
                 [dict(out=PS[:, b, 0:256], lhsT=hT[:, c, i * 128:(i + 1) * 128], rhs=w[:, c, :],
                       start=(c == 0), stop=(c == 7)) for c in range(8)])
            P.op('act', psk(b), [('V', tl, vb)],
                 lambda e: e.copy(out=V[:, tl, vb * 256:(vb + 1) * 256], in_=PS[:, b, 0:256]))

    def pool_block(l, w, wkey, tg):
        pw = smw[:, 0:256].rearrange("p (f d) -> p f d", f=2)
        for tc in range(2):
            gc = tg * 2 + tc
            for fc in range(2):
                if gc == 0:
                    P.op('dve', [], [('up', fc)], lambda e: e.memset(up[:, fc, 0:16], 0.0))
                else:
                    P.op('dve', [], [('up', fc)],
                         lambda e: e.tensor_copy(out=up[:, fc, 0:16], in_=up[:, fc, CH:CH + 16]))
                b = palloc()
                fm_matmul(b, w, wkey, fc * 128, tc)
                P.op('act', psk(b), [('up', fc)], lambda e: e.copy(out=up[:, fc, 16:16 + CH], in_=PS[:, b, :]))
                u = up[:, fc, :]
                W_ = 16 + CH
                lv = []
                prev, prevk = u, ('up', fc)
                nlev = 2 if fc == 0 else 4
                for li in range(nlev):
                    sh = 1 << li
                    lo = 2 * sh - 1
                    t_, tk = talloc()
                    P.op('dve', [prevk], [tk],
                         lambda e: e.tensor_tensor(out=t_[:, lo:W_], in0=prev[:, lo:W_], in1=prev[:, lo - sh:W_ - sh],
                                                   op=ALU.add))
                    lv.append((t_, tk))
                    prev, prevk = t_, tk
                pl, plk = talloc()
                plb = pl[:, 0:CH // 2].bitcast(BF16)
                for half in range(2):
                    g = fc * 2 + half
                    wdw = 2 << g
                    s_, sk = lv[g]
                    ps_ = slice(half * 64, (half + 1) * 64)
                    if gc == 0:
                        P.op('dve', [sk, 'cf'], [sk],
                             lambda e: e.tensor_tensor(out=s_[ps_, 16:32], in0=s_[ps_, 16:32],
                                                       in1=cf[ps_, KF_FIX + fc * 16:KF_FIX + fc * 16 + 16],
                                                       op=ALU.mult))
                    P.op('dve', [sk, ('up', fc)], [plk],
                         lambda e: e.scalar_tensor_tensor(out=plb[ps_, :], in0=s_[ps_, 16:16 + CH],
                                                          scalar=1.0 / wdw, in1=u[ps_, 16:16 + CH],
                                                          op0=ALU.mult, op1=ALU.subtract))
                b2 = palloc()
                P.mm([plk, 'smw'], psk(b2), [dict(out=PS[:, b2, :], lhsT=pw[:, fc, :], rhs=plb, start=True, stop=True)])
                P.op('act', psk(b2) + ['cols'], [('mix', 4 + fc, tc)],
                     lambda e: e.activation(out=MX[:, 4 + fc, tc * CH:(tc + 1) * CH], in_=PS[:, b2, :], func=AF.Copy,
                                            scale=cols[:, l, C_PS + fc:C_PS + fc + 1]))

    def conv_glu(l, wa_, wak, wg_, wgk, tg):
        for tc in range(2):
            gc = tg * 2 + tc
            for fc in range(2):
                dst = hc2[:, tc, fc, :]
                if gc == 0:
                    P.op('dve', [], [('hc', tc, fc)], lambda e: e.memset(dst[:, 0:30], 0.0))
                else:
                    P.op('dve', [('hc', 1 - tc, fc)], [('hc', tc, fc)],
                         lambda e: e.tensor_copy(out=dst[:, 0:30], in_=hc2[:, 1 - tc, fc, CH:CH + 30]))
                ba = palloc()
                fm_matmul(ba, wa_, wak, fc * 128, tc)
                bg = palloc()
                fm_matmul(bg, wg_, wgk, fc * 128, tc)
                sg, sgk = talloc()
                P.op('act', psk(bg), [sgk], lambda e: e.activation(out=sg[:, 0:CH], in_=PS[:, bg, :], func=AF.Sigmoid))
                P.op('dve', psk(ba) + [sgk], [('hc', tc, fc)],
                     lambda e: e.tensor_tensor(out=dst[:, 30:30 + CH], in0=PS[:, ba, :], in1=sg[:, 0:CH],
                                               op=ALU.mult))

    def conv_side(l, tg):
        cpw = smw[:, 256:768].rearrange("p (f d) -> p f d", f=2)
        T8, T9, T10 = tmp[8], tmp[9], tmp[10]
        K8 = [('tmp', 8, 0), ('tmp', 8, 1)]
        K9 = [('tmp', 9, 0), ('tmp', 9, 1)]
        K10 = [('tmp', 10, 0), ('tmp', 10, 1)]
        B6 = 6
        ndg = 0

        def OP(eng, reads, writes, fn):
            return ('op', lambda: P.op(eng, reads, writes, fn))

        for tc in range(2):
            accs = []
            for fc in range(2):
                src = hc2[:, tc, fc, :]
                hk = ('hc', tc, fc)
                cw = cols[:, l, C_CW + fc * CONVK:C_CW + (fc + 1) * CONVK]
                for k in range(CONVK):
                    di = ndg % 4
                    ndg += 1
                    yield ('free', lambda di=di, k=k, cw=cw: P.op(
                        'dve', ['cbf', 'cols'], [('dg', di)],
                        lambda e: e.tensor_scalar(out=dgb[:, di, :], in0=ident, scalar1=cw[:, k:k + 1],
                                                  scalar2=None, op0=ALU.mult)))
                    yield ('pe', ([('dg', di), hk], psk(B6),
                                  [dict(out=PS[:, B6, :], lhsT=dgb[:, di, :], rhs=src[:, k:k + CH], start=(k == 0),
                                        stop=(k == CONVK - 1), skip_group_check=True)]))
                a_ = tmp[11 + fc][:, 0:CH]
                ak = [('tmp', 11 + fc, 0), ('tmp', 11 + fc, 1)]
                yield OP('dve', psk(B6) + ['cols'], [ak],
                         lambda e, a_=a_, fc=fc: e.tensor_scalar(out=a_, in0=PS[:, B6, :],
                                                                 scalar1=cols[:, l, C_CB + fc:C_CB + fc + 1],
                                                                 scalar2=None, op0=ALU.add))
                accs.append((a_, ak))
            aks = [accs[0][1], accs[1][1]]
            yield ('pe', (aks + ['cf'], psk(B6),
                          [dict(out=PS[:, B6, :], lhsT=o256, rhs=accs[fc][0], start=(fc == 0), stop=(fc == 1))
                           for fc in range(2)]))
            yield OP('dve', psk(B6), [K9], lambda e: e.tensor_copy(out=T9[:, 0:CH], in_=PS[:, B6, :]))
            for fc in range(2):
                a_, ak = accs[fc]
                yield OP('dve', [ak, K9], [ak],
                         lambda e, a_=a_: e.tensor_tensor(out=a_, in0=a_, in1=T9[:, 0:CH], op=ALU.subtract))
            for fc in range(2):
                a_, ak = accs[fc]
                yield OP('dve', [ak], [K8],
                         lambda e, a_=a_: e.tensor_tensor(out=T8[:, 0:CH], in0=a_, in1=a_, op=ALU.mult))
                yield ('pe', ([K8, 'cf'], psk(B6),
                              [dict(out=PS[:, B6, :], lhsT=o256, rhs=T8[:, 0:CH], start=(fc == 0), stop=(fc == 1),
                                    skip_group_check=True)]))
            yield OP('act', psk(B6), [K10],
                     lambda e: e.activation(out=T10[:, 0:CH], in_=PS[:, B6, :], func=AF.Ln, bias=epsm5))
            yield OP('act', [K10], [K10],
                     lambda e: e.activation(out=T10[:, 0:CH], in_=T10[:, 0:CH], func=AF.Exp, scale=-0.5))
            ysl = []
            for fc in range(2):
                a_, ak = accs[fc]
                yield OP('dve', [ak, K10], [ak],
                         lambda e, a_=a_: e.tensor_tensor(out=a_, in0=a_, in1=T10[:, 0:CH], op=ALU.mult))
                ysb = T8[:, fc * 272:fc * 272 + 256].bitcast(BF16)
                ysk = ('tmp', 8, fc)
                yield OP('act', [ak, 'cols'], [ysk],
                         lambda e, a_=a_, ysb=ysb, fc=fc: e.activation(
                             out=ysb, in_=a_, func=AF.Silu, scale=cols[:, l, C_LG + fc:C_LG + fc + 1],
                             bias=cols[:, l, C_LB + fc:C_LB + fc + 1]))
                ysl.append((ysb, ysk))
            for fo in range(2):
                yield ('pe', ([ysl[0][1], ysl[1][1], 'smw'], psk(B6),
                              [dict(out=PS[:, B6, :], lhsT=cpw[:, fc, fo * 128:(fo + 1) * 128], rhs=ysl[fc][0],
                                    start=(fc == 0), stop=(fc == 1)) for fc in range(2)]))
                yield OP('dve', psk(B6), [('mix', 6 + fo, tc)],
                         lambda e, fo=fo, tc=tc: e.tensor_copy(out=MX[:, 6 + fo, tc * CH:(tc + 1) * CH],
                                                               in_=PS[:, B6, :]))

    def attention(tg, side=None):
        its = []
        for tc in range(2):
            gq = tg * 2 + tc
            i0 = 4 * gq
            for pair in range(4):
                ob = 4 + ((gq * 4 + pair) % 2)
                for kb in range(i0 + 3, -1, -1):
                    its.append(dict(tc=tc, gq=gq, i0=i0, pair=pair, kb=kb,
                                    first=(kb == i0 + 3), last=(kb == 0), ob=ob))
        N = len(its)

        def tk(i):
            return [('tmp', i, 0), ('tmp', i, 1)]

        def bfpair(i):
            return tmpool[:, i, :].bitcast(BF16).rearrange("p (h c) -> p h c", h=2)[:, :, 0:CH]
        Ef = tmpool[:, 0:2, 0:CH]
        KE = tk(0) + tk(1)
        SPb = [bfpair(2), bfpair(3)]
        KSP = [tk(2), tk(3)]
        Wb = [bfpair(4), bfpair(5), bfpair(6)]
        KW = [tk(4), tk(5), tk(6)]
        Ab = bfpair(7)
        KA = tk(7)
        ZB, SB = 0, 2

        def geo(d):
            c0 = max(0, d['kb'] - d['i0']) * 128
            diag = d['kb'] >= d['i0']
            return c0, diag

        def qk_pair(d, bank, start):
            c0, diag = geo(d)
            fc = d['pair']
            q0 = d['tc'] * CH
            mms = []
            for hp in range(2):
                pr = slice(hp * 64, (hp + 1) * 64)
                mms.append(dict(out=PS[:, bank + hp, c0:CH], lhsT=kT[pr, fc, d['kb'] * 128:(d['kb'] + 1) * 128],
                                rhs=qT[pr, fc, q0 + c0:q0 + CH], start=start, stop=not diag))
            if diag:
                for hp in range(2):
                    mms.append(dict(out=PS[:, bank + hp, c0:c0 + 128], lhsT=ident, rhs=negm, start=False, stop=True))
            return mms

        def stA(n):
            d = its[n]
            fc = d['pair']
            pegroups.append(([('kT', fc, d['kb'] // 4), ('qT', fc, d['tc']), 'cbf'], psk(ZB) + psk(ZB + 1),
                             qk_pair(d, ZB, True)))
            if d['first']:
                pegroups.append((['cbf', 'z512'], psk(d['ob']),
                                 [dict(out=PS[:, d['ob'], :], lhsT=ones, rhs=z512, start=True, stop=False,
                                       skip_group_check=True)]))

        def stB(n):
            d = its[n]
            c0, diag = geo(d)
            P.op('act', psk(ZB) + psk(ZB + 1), [KE],
                 lambda e: e.activation(out=Ef[:, :, c0:CH], in_=PS[:, ZB:ZB + 2, c0:CH], func=AF.Exp))
            P.op('act', [KE], [KSP[n % 2]],
                 lambda e: e.activation(out=SPb[n % 2][:, :, c0:CH], in_=Ef[:, :, c0:CH], func=AF.Ln, bias=1.0))

        def stC(n):
            d = its[n]
            c0, diag = geo(d)
            fc = d['pair']
            mms = []
            rd = [KSP[n % 2], 'cbf', ('kT', fc, d['kb'] // 4), ('qT', fc, d['tc'])]
            for hp in range(2):
                mms.append(dict(out=PS[:, SB + hp, c0:CH], lhsT=triI, rhs=SPb[n % 2][:, hp, c0:CH],
                                start=True, stop=False))
                if not d['first']:
                    mms.append(dict(out=PS[:, SB + hp, c0:CH], lhsT=ones, rhs=Ab[:, hp, c0:CH],
                                    start=False, stop=False))
            if not d['first']:
                rd.append(KA)
            mms += qk_pair(d, SB, False)
            pegroups.append((rd, psk(SB) + psk(SB + 1), mms))

        def stC2(n):
            d = its[n]
            c0, diag = geo(d)
            if d['first']:
                P.op(A_ENG, [], [KA], lambda e: e.memset(Ab, 0.0))
            if not d['last']:
                P.op(A_ENG, [KSP[n % 2], KA], [KA],
                     lambda e: e.tensor_tensor(out=Ab[:, :, c0:CH], in0=Ab[:, :, c0:CH],
                                               in1=SPb[n % 2][:, :, c0:CH], op=ALU.add))

        def stD(n):
            d = its[n]
            c0, diag = geo(d)
            P.op('act', psk(SB) + psk(SB + 1), [KW[n % 3]],
                 lambda e: e.activation(out=Wb[n % 3][:, :, c0:CH], in_=PS[:, SB:SB + 2, c0:CH], func=AF.Exp))

        def stE(n):
            d = its[n]
            c0, diag = geo(d)
            fc = d['pair']
            mms = []
            for hp in range(2):
                h = fc * 2 + hp
                pr = slice(hp * 64, (hp + 1) * 64)
                mms.append(dict(out=PS[pr, d['ob'], c0:CH], lhsT=V[:, d['kb'], h * 64:(h + 1) * 64],
                                rhs=Wb[n % 3][:, hp, c0:CH], start=False, stop=d['last'], skip_group_check=True))
            pegroups.append(([KW[n % 3], ('V', d['kb'], fc // 2)], psk(d['ob']), mms))

        def stE2(n):
            d = its[n]
            fc = d['pair']
            if d['last']:
                P.op('act', psk(d['ob']), [('mix', fc, d['tc'])],
                     lambda e: e.copy(out=MX[:, fc, d['tc'] * CH:(d['tc'] + 1) * CH], in_=PS[:, d['ob'], :]))

        pegroups = []
        LAG = 2
        P.mm([('hT', 0), 'cbf'], psk(7),
             [dict(out=PS[:, 7, :], lhsT=ident, rhs=hT[:, w_ % 8, 0:CH], start=True, stop=True)
              for w_ in range(NWARM)])
        for n in range(N + LAG):
            del pegroups[:]
            if n < N:
                stA(n)
                P.mm_batch(pegroups)
                del pegroups[:]
            if 0 <= n - 1 < N:
                stC(n - 1)
            if 0 <= n - LAG < N:
                stE(n - LAG)
            nfill = max(0, NFILL - len(sidest['pending']))
            pegroups.extend(sidest['pending'])
            del sidest['pending'][:]
            if nfill:
                pegroups.append(([('hT', 0), 'cbf'], psk(7),
                                 [dict(out=PS[:, 7, :], lhsT=ident, rhs=hT[:, w_ % 8, 0:CH], start=True, stop=True)
                                  for w_ in range(nfill)]))
            P.mm_batch(pegroups)
            if n < N:
                stB(n)
            if 0 <= n - 1 < N:
                stC2(n - 1)
                stD(n - 1)
            if 0 <= n - LAG < N:
                stE2(n - LAG)
            if side is not None:
                side(n, N + LAG)

    def wout_phase(tg):
        blks = [WB.take() for _ in range(8)]
        for i in range(8):
            tl = tg * 8 + i
            tc = i // 4
            for half in range(2):
                b = palloc()
                P.mm([k for (_, k) in blks] + [('mix', c, tc) for c in range(8)], psk(b),
                     [dict(out=PS[:, b, :], lhsT=MX[:, c, i * 128:(i + 1) * 128],
                           rhs=blks[c][0][:, half * CH:(half + 1) * CH], start=(c == 0), stop=(c == 7))
                      for c in range(8)])
                P.op('dve', psk(b) + [('X', tl)], [('X', tl)],
                     lambda e: e.tensor_tensor(out=X[:, tl, half * CH:(half + 1) * CH],
                                               in0=X[:, tl, half * CH:(half + 1) * CH], in1=PS[:, b, :],
                                               op=ALU.add))
        WB.release(8)

    def ffn_phase(l, tg, store_seq=None):
        for pi, (f0, f1) in enumerate(FPASS):
            nb = f1 - f0
            for s_ in range(nb):
                w, wk = WA.take()
                for tc in range(2):
                    bg = palloc()
                    fm_matmul(bg, w, wk, 0, tc)
                    bu = palloc()
                    fm_matmul(bu, w, wk, 128, tc)
                    sg, sgk = talloc()
                    P.op('act', psk(bg), [sgk], lambda e: e.activation(out=sg[:, 0:CH], in_=PS[:, bg, :], func=AF.Silu))
                    P.op('dve', psk(bu) + [sgk], [('mix', s_, tc)],
                         lambda e: e.tensor_tensor(out=MX[:, s_, tc * CH:(tc + 1) * CH], in0=PS[:, bu, :],
                                                   in1=sg[:, 0:CH], op=ALU.mult))
                WA.release(1)
            blks = [WB.take() for _ in range(nb)]
            for i in range(8):
                tl = tg * 8 + i
                tc = i // 4
                for half in range(2):
                    b = palloc()
                    P.mm([k for (_, k) in blks] + [('mix', s_, tc) for s_ in range(nb)], psk(b),
                         [dict(out=PS[:, b, :], lhsT=MX[:, s_, i * 128:(i + 1) * 128],
                               rhs=blks[s_][0][:, half * CH:(half + 1) * CH], start=(s_ == 0), stop=(s_ == nb - 1))
                          for s_ in range(nb)])
                    P.op('dve', psk(b) + [('X', tl)], [('X', tl)],
                         lambda e: e.tensor_tensor(out=X[:, tl, half * CH:(half + 1) * CH],
                                                   in0=X[:, tl, half * CH:(half + 1) * CH], in1=PS[:, b, :],
                                                   op=ALU.add))
                if store_seq is not None and pi == len(FPASS) - 1:
                    P.dma('sp', [(out_d[store_seq, tl * 128:(tl + 1) * 128, :], X[:, tl, :])],
                          [('X', tl)], [('out', store_seq, tl)], f'st{tl}')
            WB.release(nb)

    epsb = sb("epsb", [128, 2], F32)
    P.op('dve', [], ['epsb'], lambda e: e.memset(epsb[:, 0:1], 1e-6))
    P.op('dve', ['epsb'], ['epsb'], lambda e: e.memset(epsb[:, 1:2], 1e-5))
    epsm6 = epsb[:, 0:1]
    epsm5 = epsb[:, 1:2]
    P._wait('act', [P.res['epsb'][0]])

    for s in range(n_seq):
        for tl in range(16):
            P.dma('sp', [(X[:, tl, :], x_d[s, tl * 128:(tl + 1) * 128, :])], [], [('X', tl)], f'xl{tl}')
        for l in range(L):
            P.dma('pool', [(smw, smw_d[l])], [], ['smw'], 'smw')
            for tg in range(NTG):
                tiles = list(range(tg * 8, tg * 8 + 8))
                norm_phase(l, 0, tiles)
                held = None
                sidest['gen'] = None
                for blk in BLK_ORDER:
                    w, wk = WA.take()
                    if blk < 4:
                        qk_block(l, blk, w, wk, tg)
                        WA.release(1)
                    elif blk < 6:
                        v_block(blk, w, wk, tg)
                        WA.release(1)
                    elif blk == 6:
                        pool_block(l, w, wk, tg)
                        WA.release(1)
                    elif blk == 7:
                        held = (w, wk)
                    else:
                        conv_glu(l, held[0], held[1], w, wk, tg)
                        WA.release(2)
                        if SIDE_CONV:
                            sidest['gen'] = conv_side(l, tg)
                            sidest['left'] = 2 * (2 * (CONVK + 1) + 18)
                if not SIDE_CONV:
                    sidest['gen'] = conv_side(l, tg)
                    side_pull(10 ** 6)

                def side(n, total):
                    side_pull(4, defer=True)
                attention(tg, side=side if SIDE_CONV else None)
                side_flush()
                side_pull(10 ** 6)
                wout_phase(tg)
                norm_phase(l, 1, tiles)
                ffn_phase(l, tg, store_seq=(s if l == L - 1 else None))
    outk = [('out', s, tl) for s in range(n_seq) for tl in range(16)]
    P.final_wait('sp', outk)
    return nc, P


def host_prep(inputs, n_layers=L_FULL):
    L = n_layers
    f = np.float32
    w_in = np.asarray(inputs["w_in"], f)[:L]
    win = np.ascontiguousarray(w_in.reshape(L, 8, 128, NWIN, 256).transpose(0, 3, 2, 1, 4))
    wout = np.ascontiguousarray(np.asarray(inputs["w_out"], f)[:L].reshape(L, 8, 128, 1024))
    gu = np.asarray(inputs["ffn_w_gu"], f)[:L]
    g = gu[:, :, :HID].reshape(L, 8, 128, NF, 128)
    u = gu[:, :, HID:].reshape(L, 8, 128, NF, 128)
    wgu = np.ascontiguousarray(np.concatenate([g, u], axis=-1).transpose(0, 3, 2, 1, 4))
    wdn = np.ascontiguousarray(np.asarray(inputs["ffn_w_down"], f)[:L].reshape(L, NF, 128, 1024))
    smw = np.zeros((L, 128, 768), f)
    pw = np.asarray(inputs["pool_w"], f)[:L]
    for fc in range(2):
        for half in range(2):
            gi = fc * 2 + half
            smw[:, half * 64:(half + 1) * 64, fc * 128 + half * 64: fc * 128 + (half + 1) * 64] = pw[:, gi]
    cpw = np.asarray(inputs["conv_pw"], f)[:L]
    smw[:, :, 256:768] = cpw.reshape(L, 2, 128, 256).transpose(0, 2, 1, 3).reshape(L, 128, 512)
    cols = np.zeros((128, L, NCOL), f)
    cols[:, :, C_GQ] = np.tile(np.asarray(inputs["sb_q_g"], f)[:L], (1, 2)).T
    cols[:, :, C_GK] = np.tile(np.asarray(inputs["sb_k_g"], f)[:L], (1, 2)).T

    def pc(a):
        return np.asarray(a, f)[:L].reshape(L, 2, 128).transpose(2, 0, 1)
    cols[:, :, C_PS:C_PS + 2] = pc(inputs["pool_scale"])
    cols[:, :, C_CB:C_CB + 2] = pc(inputs["conv_b"])
    cols[:, :, C_LG:C_LG + 2] = pc(inputs["conv_ln_g"])
    cols[:, :, C_LB:C_LB + 2] = pc(inputs["conv_ln_b"])
    cw = np.asarray(inputs["conv_w"], f)[:L]
    cols[:, :, C_CW:C_CW + 2 * CONVK] = cw.reshape(L, CONVK, 2, 128).transpose(3, 0, 2, 1).reshape(128, L, 2 * CONVK)
    gains = np.ascontiguousarray(np.stack([np.asarray(inputs["norm_mix_g"], f)[:L],
                                           np.asarray(inputs["norm_ffn_g"], f)[:L]], axis=1))
    cbf = np.zeros((128, K_END), f)
    idx = np.arange(128)
    cbf[:, K_ID:K_ID + 128] = np.eye(128, dtype=f)
    cbf[:, K_TRI:K_TRI + 128] = (idx[:, None] >= idx[None, :]).astype(f) * -1.0
    cbf[:, K_ONE:K_ONE + 128] = -1.0
    cbf[:, K_NEG:K_NEG + 128] = np.where(idx[:, None] >= idx[None, :], -30000.0, 0.0)
    cf = np.zeros((128, KF_END), f)
    cf[:, KF_B64:KF_B64 + 128] = ((idx[:, None] // 64) == (idx[None, :] // 64)).astype(f) / 64.0
    cf[:, KF_O256:KF_O256 + 128] = 1.0 / 256.0
    for fc in range(2):
        for half in range(2):
            wdw = 2 << (fc * 2 + half)
            t = np.arange(16)
            cf[half * 64:(half + 1) * 64, KF_FIX + fc * 16:KF_FIX + fc * 16 + 16] = wdw / np.minimum(t + 1.0, wdw)
    return dict(win=win, wout=wout, wgu=wgu, wdn=wdn, smw=smw, cols=cols, gains=gains, cbf=cbf, cf=cf)


_CACHE = {}


def kernel(**inputs):
    x = np.asarray(inputs["x"], np.float32)
    shared = host_prep(inputs)
    if 'nc' not in _CACHE:
        _CACHE['nc'] = build_program()[0]
    nc = _CACHE['nc']
    in_maps = []
    for c in range(NCORES):
        m = dict(shared)
        m["x"] = np.ascontiguousarray(x[c * NSEQ_CORE:(c + 1) * NSEQ_CORE])
        in_maps.append(m)
    res = run_bass_kernel_spmd(nc, in_maps, core_ids=list(range(NCORES)))
    return np.concatenate([r["out"] for r in res.results], axis=0).astype(np.float32)
```

```python
import numpy as np
import concourse.bass as bass
import concourse.mybir as mybir
from concourse.bass_utils import run_bass_kernel_spmd

F32 = mybir.dt.float32
BF16 = mybir.dt.bfloat16
AF = mybir.ActivationFunctionType
ALU = mybir.AluOpType

D = 1024
T = 2048
L_FULL = 4
NSEQ_CORE = 4
NCORES = 8
TG = 1024
NTG = T // TG
CH = 512
HID = 2816
NF = HID // 128
IN_COLS = 2304
NWIN = IN_COLS // 256
CONVK = 31
FPASS = [(0, 8), (8, 15), (15, 22)]
C_GQ, C_GK, C_PS, C_CB, C_LG, C_LB, C_CW = 0, 1, 2, 4, 6, 8, 10
NCOL = 10 + 2 * CONVK
K_ID, K_TRI, K_ONE, K_NEG, K_END = 0, 128, 256, 384, 512
KF_B64, KF_O256, KF_FIX, KF_END = 0, 128, 256, 288
TMPW = 544
CONV_ENG = ['dve', 'dve']
NWARM = 20
NFILL = 2
SIDE_CONV = True
A_ENG = 'dve'
BLK_ORDER = [7, 8, 6, 0, 1, 2, 3, 4, 5]


class Prog:
    def __init__(self, nc):
        self.nc = nc
        self.E = {'pe': nc.tensor, 'act': nc.scalar, 'dve': nc.vector, 'pool': nc.gpsimd, 'sp': nc.sync}
        self.esem = {}
        self.ecnt = {}
        for e in ['pe', 'act', 'dve', 'pool']:
            self.esem[e] = nc.alloc_semaphore('sem_' + e)
            self.ecnt[e] = 0
        self.waited = {}
        self.res = {}
        self.dsem = {}
        self.nins = 0

    @staticmethod
    def _flat(keys):
        out = []
        for k in keys:
            if isinstance(k, list):
                out.extend(Prog._flat(k))
            else:
                out.append(k)
        return out

    def _deps(self, reads, writes):
        reads = self._flat(reads)
        writes = self._flat(writes)
        toks = []
        for k in reads:
            r = self.res.get(k)
            if r is not None and r[0] is not None:
                toks.append(r[0])
        for k in writes:
            r = self.res.get(k)
            if r is not None:
                if r[0] is not None:
                    toks.append(r[0])
                toks.extend(r[1].values())
        return toks

    def _commit(self, tok, reads, writes):
        reads = self._flat(reads)
        writes = self._flat(writes)
        for k in reads:
            r = self.res.get(k)
            if r is None:
                r = [None, {}]
                self.res[k] = r
            old = r[1].get(tok[0])
            if old is None or old[2] < tok[2]:
                r[1][tok[0]] = tok
        for k in writes:
            self.res[k] = [tok, {}]

    def _wait(self, eng, toks):
        best = {}
        for (name, sem, val) in toks:
            if eng == 'pe' and name == 'sem_pe':
                continue
            b = best.get(name)
            if b is None or b[1] < val:
                best[name] = (sem, val)
        for name, (sem, val) in best.items():
            if self.waited.get((eng, name), 0) >= val:
                continue
            self.E[eng].wait_ge(sem, val)
            self.waited[(eng, name)] = val
            self.nins += 1

    def op(self, eng, reads, writes, fn):
        self._wait(eng, self._deps(reads, writes))
        ins = fn(self.E[eng])
        self.ecnt[eng] += 1
        ins.then_inc(self.esem[eng], 1)
        tok = ('sem_' + eng, self.esem[eng], self.ecnt[eng])
        self._commit(tok, reads, writes)
        self.nins += 1
        return tok

    def mm(self, reads, writes, mms):
        self._wait('pe', self._deps(reads, writes))
        ins = None
        for m in mms:
            ins = self.nc.tensor.matmul(**m)
            self.nins += 1
        self.ecnt['pe'] += 1
        ins.then_inc(self.esem['pe'], 1)
        tok = ('sem_pe', self.esem['pe'], self.ecnt['pe'])
        self._commit(tok, reads, writes)
        return tok

    def mm_batch(self, groups):
        toks = []
        for (reads, writes, mms) in groups:
            toks += self._deps(reads, writes)
        self._wait('pe', toks)
        for (reads, writes, mms) in groups:
            ins = None
            for m in mms:
                ins = self.nc.tensor.matmul(**m)
                self.nins += 1
            self.ecnt['pe'] += 1
            ins.then_inc(self.esem['pe'], 1)
            tok = ('sem_pe', self.esem['pe'], self.ecnt['pe'])
            self._commit(tok, reads, writes)

    def pe_ops(self, reads, writes, fns):
        self._wait('pe', self._deps(reads, writes))
        ins = None
        for f in fns:
            ins = f(self.nc.tensor)
            self.nins += 1
        self.ecnt['pe'] += 1
        ins.then_inc(self.esem['pe'], 1)
        tok = ('sem_pe', self.esem['pe'], self.ecnt['pe'])
        self._commit(tok, reads, writes)
        return tok

    def dma(self, q, pairs, reads, writes, semkey):
        self._wait(q, self._deps(reads, writes))
        d = self.dsem.get(semkey)
        if d is None:
            d = [self.nc.alloc_semaphore('dma_' + semkey), 0]
            self.dsem[semkey] = d
        for (out, in_) in pairs:
            self.E[q].dma_start(out=out, in_=in_).then_inc(d[0], 16)
            d[1] += 16
            self.nins += 1
        tok = ('dma_' + semkey, d[0], d[1])
        self._commit(tok, reads, writes)
        return tok

    def final_wait(self, eng, keys):
        toks = []
        for k in self._flat(keys):
            r = self.res.get(k)
            if r is not None:
                if r[0] is not None:
                    toks.append(r[0])
                toks.extend(r[1].values())
        self._wait(eng, toks)


class WStream:
    def __init__(self, prog, name, nslots, shape, plan, queue):
        self.p = prog
        self.name = name
        self.n = nslots
        self.plan = plan
        self.queue = queue
        self.slots = [prog.nc.alloc_sbuf_tensor(f'{name}_s{i}', shape, BF16).ap() for i in range(nslots)]
        self.issued = 0
        self.taken = 0
        self.released = 0

    def key(self, i):
        return (self.name, i % self.n)

    def prefetch(self):
        while self.issued < len(self.plan) and self.issued < self.released + self.n:
            i = self.issued
            self.p.dma(self.queue, [(self.slots[i % self.n], self.plan[i])], [], [self.key(i)],
                       f'{self.name}{i % self.n}')
            self.issued += 1

    def take(self):
        i = self.taken
        assert i < self.issued, (self.name, i, self.issued)
        self.taken += 1
        return self.slots[i % self.n], self.key(i)

    def release(self, k=1):
        self.released += k
        self.prefetch()


def build_program(n_layers=L_FULL, n_seq=NSEQ_CORE, wq='pool'):
    nc = bass.Bass("TRN2", target_bir_lowering=False)
    P = Prog(nc)
    L = n_layers

    x_d = nc.dram_tensor("x", [n_seq, T, D], F32, kind="ExternalInput").ap()
    win_d = nc.dram_tensor("win", [L, NWIN, 128, 8, 256], F32, kind="ExternalInput").ap()
    wout_d = nc.dram_tensor("wout", [L, 8, 128, 1024], F32, kind="ExternalInput").ap()
    wgu_d = nc.dram_tensor("wgu", [L, NF, 128, 8, 256], F32, kind="ExternalInput").ap()
    wdn_d = nc.dram_tensor("wdn", [L, NF, 128, 1024], F32, kind="ExternalInput").ap()
    smw_d = nc.dram_tensor("smw", [L, 128, 768], F32, kind="ExternalInput").ap()
    cols_d = nc.dram_tensor("cols", [128, L, NCOL], F32, kind="ExternalInput").ap()
    gains_d = nc.dram_tensor("gains", [L, 2, D], F32, kind="ExternalInput").ap()
    cbf_d = nc.dram_tensor("cbf", [128, K_END], F32, kind="ExternalInput").ap()
    cf_d = nc.dram_tensor("cf", [128, KF_END], F32, kind="ExternalInput").ap()
    out_d = nc.dram_tensor("out", [n_seq, T, D], F32, kind="ExternalOutput").ap()

    def sb(name, shape, dt):
        return nc.alloc_sbuf_tensor(name, shape, dt).ap()

    X = sb("X", [128, 16, D], F32)
    hT = sb("hT", [128, 8, TG], BF16)
    kT = sb("kT", [128, 4, T], BF16)
    V = sb("V", [128, 16, 512], BF16)
    qT = sb("qT", [128, 4, TG], BF16)
    MX = sb("MX", [128, 8, TG], BF16)
    NTMP = 11
    NROT = 8
    tmpool = sb("tmpool", [128, NTMP, TMPW], F32)
    tmp = [tmpool[:, i, :] for i in range(NTMP)]
    up = sb("up", [128, 2, 16 + CH], F32)
    hc2 = sb("hc2", [128, 2, 2, 30 + CH], F32)
    gain = tmpool[:, 9:11, 0:512]
    KGAIN = [('tmp', 9, 0), ('tmp', 9, 1), ('tmp', 10, 0), ('tmp', 10, 1)]
    cbf = sb("cbf_sb", [128, K_END], BF16)
    z512 = sb("z512", [128, CH], BF16)
    cf = sb("cf_sb", [128, KF_END], F32)
    cols = sb("cols_sb", [128, L, NCOL], F32)
    gq8 = sb("gq8", [128, L], F32)
    smw = sb("smw_sb", [128, 768], BF16)
    ss = sb("ss", [128, 8], F32)
    rs = sb("rs", [128, 8], F32)
    rs2 = sb("rs2", [128, 8], F32)
    PS = nc.alloc_psum_tensor("PS", [128, 8, 512], F32).ap()

    ident = cbf[:, K_ID:K_ID + 128]
    triI = cbf[:, K_TRI:K_TRI + 128]
    ones = cbf[:, K_ONE:K_ONE + 128]
    negm = cbf[:, K_NEG:K_NEG + 128]
    b64 = cf[:, KF_B64:KF_B64 + 128]
    o256 = cf[:, KF_O256:KF_O256 + 128]

    def psk(b):
        return [('ps', b, 0), ('ps', b, 1)]

    tstate = {'i': 0}

    def talloc():
        i = tstate['i'] % NROT
        tstate['i'] += 1
        return tmp[i], [('tmp', i, 0), ('tmp', i, 1)]

    pstate = {'i': 0}

    def palloc(banks=(0, 1, 2, 3, 4, 5, 7)):
        b = banks[pstate['i'] % len(banks)]
        pstate['i'] += 1
        return b

    planA, planB = [], []
    for s in range(n_seq):
        for l in range(L):
            for tg in range(NTG):
                for j in BLK_ORDER:
                    planA.append(win_d[l, j])
                for c in range(8):
                    planB.append(wout_d[l, c])
                for (f0, f1) in FPASS:
                    for f in range(f0, f1):
                        planA.append(wgu_d[l, f])
                    for f in range(f0, f1):
                        planB.append(wdn_d[l, f])
    WA = WStream(P, 'wa', 3, [128, 8, 256], planA, wq)
    WB = WStream(P, 'wb', 8, [128, 1024], planB, wq)

    P.dma('pool', [(cbf, cbf_d)], [], ['cbf'], 'c0')
    P.dma('sp', [(cf, cf_d)], [], ['cf'], 'c1')
    P.dma('sp', [(cols, cols_d)], [], ['cols'], 'c2')
    P.op('pool', [], ['z512'], lambda e: e.memset(z512, 0.0))
    P.op('act', ['cols'], ['gq8'], lambda e: e.mul(out=gq8, in_=cols[:, :, C_GQ], mul=0.125))
    WA.prefetch()
    WB.prefetch()

    sidest = {'gen': None, 'left': 0}

    def side_pull(k):
        g = sidest['gen']
        if g is None:
            return
        for _ in range(k):
            try:
                next(g)
                sidest['left'] -= 1
            except StopIteration:
                sidest['gen'] = None
                sidest['left'] = 0
                return

    def norm_phase(l, which, tiles):
        P.dma('sp', [(gain, gains_d[l, which].partition_broadcast(128).rearrange("p (a b) -> p a b", a=2))],
              [], [KGAIN], 'gain')
        for i, tl in enumerate(tiles):
            jk, jkk = talloc()
            P.op('act', [('X', tl)], [jkk, ('ss', i)],
                 lambda e: e.activation(out=jk[:, 0:512].bitcast(BF16), in_=X[:, tl, :], func=AF.Square,
                                        accum_out=ss[:, i:i + 1]))
        ssk = [('ss', i) for i in range(8)]
        P.op('act', ssk, ['rs'], lambda e: e.activation(out=rs, in_=ss, func=AF.Ln, scale=1.0 / D, bias=epsm6))
        P.op('act', ['rs'], ['rs2'], lambda e: e.activation(out=rs2, in_=rs, func=AF.Exp, scale=-0.5))
        for i, tl in enumerate(tiles):
            xt_, xnk = talloc()
            xb = xt_[:, 0:512].bitcast(BF16)
            P.op('dve', [('X', tl), 'rs2', KGAIN], [xnk],
                 lambda e: e.scalar_tensor_tensor(out=xb.rearrange("p (a b) -> p a b", a=2),
                                                  in0=X[:, tl, :].rearrange("p (a b) -> p a b", a=2),
                                                  scalar=rs2[:, i:i + 1], in1=gain,
                                                  op0=ALU.mult, op1=ALU.mult))
            b = palloc()
            pb = PS[:, b, :].bitcast(BF16)
            P.pe_ops([xnk, 'cbf'], psk(b),
                     [(lambda e, c=c: e.transpose(out=pb[:, c * 128:(c + 1) * 128],
                                                  in_=xb[:, c * 128:(c + 1) * 128], identity=ident))
                      for c in range(8)])
            P.op('act', psk(b), [('hT', i)],
                 lambda e: e.copy(out=hT[:, :, i * 128:(i + 1) * 128],
                                  in_=pb.rearrange("p (c t) -> p c t", c=8)))

    def fm_matmul(b, w, wkey, col0, tc):
        P.mm([wkey] + [('hT', tc * 4 + i) for i in range(4)], psk(b),
             [dict(out=PS[:, b, :], lhsT=w[:, c, col0:col0 + 128], rhs=hT[:, c, tc * CH:(tc + 1) * CH],
                   start=(c == 0), stop=(c == 7)) for c in range(8)])

    def qk_block(l, blk, w, wkey, tg):
        isq = blk < 2
        for f2 in range(2):
            fc = (blk % 2) * 2 + f2
            gcol = gq8[:, l:l + 1] if isq else cols[:, l, C_GK:C_GK + 1]
            for tc in range(2):
                b = palloc()
                fm_matmul(b, w, wkey, f2 * 128, tc)
                sq, sqk = talloc()
                P.op('act', psk(b), [sqk], lambda e: e.activation(out=sq[:, 0:CH], in_=PS[:, b, :], func=AF.Square))
                b2 = palloc()
                P.mm([sqk, 'cf'], psk(b2), [dict(out=PS[:, b2, :], lhsT=b64, rhs=sq[:, 0:CH], start=True, stop=True)])
                sd, sdk = talloc()
                P.op('act', psk(b2), [sdk],
                     lambda e: e.activation(out=sd[:, 0:CH], in_=PS[:, b2, :], func=AF.Ln, bias=epsm6))
                ri, rik = sd, sdk
                P.op('act', [sdk], [sdk],
                     lambda e: e.activation(out=sd[:, 0:CH], in_=sd[:, 0:CH], func=AF.Exp, scale=-0.5))
                if isq:
                    dst = qT[:, fc, tc * CH:(tc + 1) * CH]
                    dk = ('qT', fc, tc)
                else:
                    g0 = tg * TG + tc * CH
                    dst = kT[:, fc, g0:g0 + CH]
                    dk = ('kT', fc, tg * 2 + tc)
                P.op('dve', psk(b) + [rik, 'cols', 'gq8'], [dk],
                     lambda e: e.scalar_tensor_tensor(out=dst, in0=PS[:, b, :], scalar=gcol, in1=ri[:, 0:CH],
                                                      op0=ALU.mult, op1=ALU.mult))
                side_pull(2)

    def v_block(blk, w, wkey, tg):
        vb = blk - 4
        for i in range(8):
            tl = tg * 8 + i
            b = palloc()
            P.mm([wkey, ('hT', i)], psk(b),
                 [dict(out=PS[:, b, 0:256], lhsT=hT[:, c, i * 128:(i + 1) * 128], rhs=w[:, c, :],
                       start=(c == 0), stop=(c == 7)) for c in range(8)])
            P.op('act', psk(b), [('V', tl, vb)],
                 lambda e: e.copy(out=V[:, tl, vb * 256:(vb + 1) * 256], in_=PS[:, b, 0:256]))
            side_pull(2)

    def pool_block(l, w, wkey, tg):
        pw = smw[:, 0:256].rearrange("p (f d) -> p f d", f=2)
        for tc in range(2):
            gc = tg * 2 + tc
            for fc in range(2):
                if gc == 0:
                    P.op('dve', [], [('up', fc)], lambda e: e.memset(up[:, fc, 0:16], 0.0))
                else:
                    P.op('dve', [], [('up', fc)],
                         lambda e: e.tensor_copy(out=up[:, fc, 0:16], in_=up[:, fc, CH:CH + 16]))
                b = palloc()
                fm_matmul(b, w, wkey, fc * 128, tc)
                P.op('act', psk(b), [('up', fc)], lambda e: e.copy(out=up[:, fc, 16:16 + CH], in_=PS[:, b, :]))
                u = up[:, fc, :]
                W_ = 16 + CH
                lv = []
                prev, prevk = u, ('up', fc)
                nlev = 2 if fc == 0 else 4
                for li in range(nlev):
                    sh = 1 << li
                    lo = 2 * sh - 1
                    t_, tk = talloc()
                    P.op('dve', [prevk], [tk],
                         lambda e: e.tensor_tensor(out=t_[:, lo:W_], in0=prev[:, lo:W_], in1=prev[:, lo - sh:W_ - sh],
                                                   op=ALU.add))
                    lv.append((t_, tk))
                    prev, prevk = t_, tk
                pl, plk = talloc()
                plb = pl[:, 0:CH // 2].bitcast(BF16)
                for half in range(2):
                    g = fc * 2 + half
                    wdw = 2 << g
                    s_, sk = lv[g]
                    ps_ = slice(half * 64, (half + 1) * 64)
                    if gc == 0:
                        P.op('dve', [sk, 'cf'], [sk],
                             lambda e: e.tensor_tensor(out=s_[ps_, 16:32], in0=s_[ps_, 16:32],
                                                       in1=cf[ps_, KF_FIX + fc * 16:KF_FIX + fc * 16 + 16],
                                                       op=ALU.mult))
                    P.op('dve', [sk, ('up', fc)], [plk],
                         lambda e: e.scalar_tensor_tensor(out=plb[ps_, :], in0=s_[ps_, 16:16 + CH],
                                                          scalar=1.0 / wdw, in1=u[ps_, 16:16 + CH],
                                                          op0=ALU.mult, op1=ALU.subtract))
                b2 = palloc()
                P.mm([plk, 'smw'], psk(b2), [dict(out=PS[:, b2, :], lhsT=pw[:, fc, :], rhs=plb, start=True, stop=True)])
                P.op('act', psk(b2) + ['cols'], [('mix', 4 + fc, tc)],
                     lambda e: e.activation(out=MX[:, 4 + fc, tc * CH:(tc + 1) * CH], in_=PS[:, b2, :], func=AF.Copy,
                                            scale=cols[:, l, C_PS + fc:C_PS + fc + 1]))

    def conv_glu(l, wa_, wak, wg_, wgk, tg):
        for tc in range(2):
            gc = tg * 2 + tc
            for fc in range(2):
                dst = hc2[:, tc, fc, :]
                if gc == 0:
                    P.op('dve', [], [('hc', tc, fc)], lambda e: e.memset(dst[:, 0:30], 0.0))
                else:
                    P.op('dve', [('hc', 1 - tc, fc)], [('hc', tc, fc)],
                         lambda e: e.tensor_copy(out=dst[:, 0:30], in_=hc2[:, 1 - tc, fc, CH:CH + 30]))
                ba = palloc()
                fm_matmul(ba, wa_, wak, fc * 128, tc)
                bg = palloc()
                fm_matmul(bg, wg_, wgk, fc * 128, tc)
                sg, sgk = talloc()
                P.op('act', psk(bg), [sgk], lambda e: e.activation(out=sg[:, 0:CH], in_=PS[:, bg, :], func=AF.Sigmoid))
                P.op('dve', psk(ba) + [sgk], [('hc', tc, fc)],
                     lambda e: e.tensor_tensor(out=dst[:, 30:30 + CH], in0=PS[:, ba, :], in1=sg[:, 0:CH],
                                               op=ALU.mult))

    def conv_side(l, tg):
        cpw = smw[:, 256:768].rearrange("p (f d) -> p f d", f=2)
        T8, T9, T10 = tmp[8], tmp[9], tmp[10]
        K8 = [('tmp', 8, 0), ('tmp', 8, 1)]
        K9 = [('tmp', 9, 0), ('tmp', 9, 1)]
        K10 = [('tmp', 10, 0), ('tmp', 10, 1)]
        B6 = 6
        for tc in range(2):
            accs = []
            for fc in range(2):
                src = hc2[:, tc, fc, :]
                hk = ('hc', tc, fc)
                cw = cols[:, l, C_CW + fc * CONVK:C_CW + (fc + 1) * CONVK]
                P.op('dve', [hk, 'cols'], [K8],
                     lambda e: e.tensor_scalar(out=T8[:, 0:CH], in0=src[:, 0:CH], scalar1=cw[:, 0:1],
                                               scalar2=cols[:, l, C_CB + fc:C_CB + fc + 1],
                                               op0=ALU.mult, op1=ALU.add))
                yield
                for k in range(1, CONVK):
                    P.op('dve', [hk, 'cols', K8], [K8],
                         lambda e: e.scalar_tensor_tensor(out=T8[:, 0:CH], in0=src[:, k:k + CH],
                                                          scalar=cw[:, k:k + 1], in1=T8[:, 0:CH],
                                                          op0=ALU.mult, op1=ALU.add))
                    yield
                P.op('dve', [K8], [hk], lambda e: e.tensor_copy(out=src[:, 0:CH], in_=T8[:, 0:CH]))
                yield
                accs.append((src[:, 0:CH], hk))
            aks = [accs[0][1], accs[1][1]]
            P.mm(aks + ['cf'], psk(B6),
                 [dict(out=PS[:, B6, :], lhsT=o256, rhs=accs[fc][0], start=(fc == 0), stop=(fc == 1))
                  for fc in range(2)])
            yield
            P.op('dve', psk(B6), [K9], lambda e: e.tensor_copy(out=T9[:, 0:CH], in_=PS[:, B6, :]))
            yield
            for fc in range(2):
                a_, ak = accs[fc]
                P.op('dve', [ak, K9], [ak],
                     lambda e: e.tensor_tensor(out=a_, in0=a_, in1=T9[:, 0:CH], op=ALU.subtract))
                yield
            for fc in range(2):
                a_, ak = accs[fc]
                P.op('dve', [ak], [K8], lambda e: e.tensor_tensor(out=T8[:, 0:CH], in0=a_, in1=a_, op=ALU.mult))
                yield
                P.mm([K8, 'cf'], psk(B6),
                     [dict(out=PS[:, B6, :], lhsT=o256, rhs=T8[:, 0:CH], start=(fc == 0), stop=(fc == 1),
                           skip_group_check=True)])
                yield
            P.op('act', psk(B6), [K10],
                 lambda e: e.activation(out=T10[:, 0:CH], in_=PS[:, B6, :], func=AF.Ln, bias=epsm5))
            yield
            P.op('act', [K10], [K10],
                 lambda e: e.activation(out=T10[:, 0:CH], in_=T10[:, 0:CH], func=AF.Exp, scale=-0.5))
            yield
            ysl = []
            for fc in range(2):
                a_, ak = accs[fc]
                P.op('dve', [ak, K10], [ak],
                     lambda e: e.tensor_tensor(out=a_, in0=a_, in1=T10[:, 0:CH], op=ALU.mult))
                yield
                ysb = T8[:, fc * 272:fc * 272 + 256].bitcast(BF16)
                ysk = ('tmp', 8, fc)
                P.op('act', [ak, 'cols'], [ysk],
                     lambda e: e.activation(out=ysb, in_=a_, func=AF.Silu,
                                            scale=cols[:, l, C_LG + fc:C_LG + fc + 1],
                                            bias=cols[:, l, C_LB + fc:C_LB + fc + 1]))
                yield
                ysl.append((ysb, ysk))
            for fo in range(2):
                P.mm([ysl[0][1], ysl[1][1], 'smw'], psk(B6),
                     [dict(out=PS[:, B6, :], lhsT=cpw[:, fc, fo * 128:(fo + 1) * 128], rhs=ysl[fc][0],
                           start=(fc == 0), stop=(fc == 1)) for fc in range(2)])
                yield
                P.op('dve', psk(B6), [('mix', 6 + fo, tc)],
                     lambda e: e.tensor_copy(out=MX[:, 6 + fo, tc * CH:(tc + 1) * CH], in_=PS[:, B6, :]))
                yield

    def attention(tg, side=None):
        its = []
        for tc in range(2):
            gq = tg * 2 + tc
            i0 = 4 * gq
            for pair in range(4):
                ob = 4 + ((gq * 4 + pair) % 2)
                for kb in range(i0 + 3, -1, -1):
                    its.append(dict(tc=tc, gq=gq, i0=i0, pair=pair, kb=kb,
                                    first=(kb == i0 + 3), last=(kb == 0), ob=ob))
        N = len(its)

        def tk(i):
            return [('tmp', i, 0), ('tmp', i, 1)]

        def bfpair(i):
            return tmpool[:, i, :].bitcast(BF16).rearrange("p (h c) -> p h c", h=2)[:, :, 0:CH]
        Ef = tmpool[:, 0:2, 0:CH]
        KE = tk(0) + tk(1)
        SPb = [bfpair(2), bfpair(3)]
        KSP = [tk(2), tk(3)]
        Wb = [bfpair(4), bfpair(5), bfpair(6)]
        KW = [tk(4), tk(5), tk(6)]
        Ab = bfpair(7)
        KA = tk(7)
        ZB, SB = 0, 2

        def geo(d):
            c0 = max(0, d['kb'] - d['i0']) * 128
            diag = d['kb'] >= d['i0']
            return c0, diag

        def qk_pair(d, bank, start):
            c0, diag = geo(d)
            fc = d['pair']
            q0 = d['tc'] * CH
            mms = []
            for hp in range(2):
                pr = slice(hp * 64, (hp + 1) * 64)
                mms.append(dict(out=PS[:, bank + hp, c0:CH], lhsT=kT[pr, fc, d['kb'] * 128:(d['kb'] + 1) * 128],
                                rhs=qT[pr, fc, q0 + c0:q0 + CH], start=start, stop=not diag))
            if diag:
                for hp in range(2):
                    mms.append(dict(out=PS[:, bank + hp, c0:c0 + 128], lhsT=ident, rhs=negm, start=False, stop=True))
            return mms

        def stA(n):
            d = its[n]
            fc = d['pair']
            pegroups.append(([('kT', fc, d['kb'] // 4), ('qT', fc, d['tc']), 'cbf'], psk(ZB) + psk(ZB + 1),
                             qk_pair(d, ZB, True)))
            if d['first']:
                pegroups.append((['cbf', 'z512'], psk(d['ob']),
                                 [dict(out=PS[:, d['ob'], :], lhsT=ones, rhs=z512, start=True, stop=False,
                                       skip_group_check=True)]))

        def stB(n):
            d = its[n]
            c0, diag = geo(d)
            P.op('act', psk(ZB) + psk(ZB + 1), [KE],
                 lambda e: e.activation(out=Ef[:, :, c0:CH], in_=PS[:, ZB:ZB + 2, c0:CH], func=AF.Exp))
            P.op('act', [KE], [KSP[n % 2]],
                 lambda e: e.activation(out=SPb[n % 2][:, :, c0:CH], in_=Ef[:, :, c0:CH], func=AF.Ln, bias=1.0))

        def stC(n):
            d = its[n]
            c0, diag = geo(d)
            fc = d['pair']
            mms = []
            rd = [KSP[n % 2], 'cbf', ('kT', fc, d['kb'] // 4), ('qT', fc, d['tc'])]
            for hp in range(2):
                mms.append(dict(out=PS[:, SB + hp, c0:CH], lhsT=triI, rhs=SPb[n % 2][:, hp, c0:CH],
                                start=True, stop=False))
                if not d['first']:
                    mms.append(dict(out=PS[:, SB + hp, c0:CH], lhsT=ones, rhs=Ab[:, hp, c0:CH],
                                    start=False, stop=False))
            if not d['first']:
                rd.append(KA)
            mms += qk_pair(d, SB, False)
            pegroups.append((rd, psk(SB) + psk(SB + 1), mms))

        def stC2(n):
            d = its[n]
            c0, diag = geo(d)
            if d['first']:
                P.op(A_ENG, [], [KA], lambda e: e.memset(Ab, 0.0))
            if not d['last']:
                P.op(A_ENG, [KSP[n % 2], KA], [KA],
                     lambda e: e.tensor_tensor(out=Ab[:, :, c0:CH], in0=Ab[:, :, c0:CH],
                                               in1=SPb[n % 2][:, :, c0:CH], op=ALU.add))

        def stD(n):
            d = its[n]
            c0, diag = geo(d)
            P.op('act', psk(SB) + psk(SB + 1), [KW[n % 3]],
                 lambda e: e.activation(out=Wb[n % 3][:, :, c0:CH], in_=PS[:, SB:SB + 2, c0:CH], func=AF.Exp))

        def stE(n):
            d = its[n]
            c0, diag = geo(d)
            fc = d['pair']
            mms = []
            for hp in range(2):
                h = fc * 2 + hp
                pr = slice(hp * 64, (hp + 1) * 64)
                mms.append(dict(out=PS[pr, d['ob'], c0:CH], lhsT=V[:, d['kb'], h * 64:(h + 1) * 64],
                                rhs=Wb[n % 3][:, hp, c0:CH], start=False, stop=d['last'], skip_group_check=True))
            pegroups.append(([KW[n % 3], ('V', d['kb'], fc // 2)], psk(d['ob']), mms))

        def stE2(n):
            d = its[n]
            fc = d['pair']
            if d['last']:
                P.op('act', psk(d['ob']), [('mix', fc, d['tc'])],
                     lambda e: e.copy(out=MX[:, fc, d['tc'] * CH:(d['tc'] + 1) * CH], in_=PS[:, d['ob'], :]))

        pegroups = []
        LAG = 2
        P.mm([('hT', 0), 'cbf'], psk(7),
             [dict(out=PS[:, 7, :], lhsT=ident, rhs=hT[:, w_ % 8, 0:CH], start=True, stop=True)
              for w_ in range(NWARM)])
        for n in range(N + LAG):
            del pegroups[:]
            if n < N:
                stA(n)
                P.mm_batch(pegroups)
                del pegroups[:]
            if 0 <= n - 1 < N:
                stC(n - 1)
            if 0 <= n - LAG < N:
                stE(n - LAG)
            if NFILL:
                pegroups.append(([('hT', 0), 'cbf'], psk(7),
                                 [dict(out=PS[:, 7, :], lhsT=ident, rhs=hT[:, w_ % 8, 0:CH], start=True, stop=True)
                                  for w_ in range(NFILL)]))
            P.mm_batch(pegroups)
            if n < N:
                stB(n)
            if 0 <= n - 1 < N:
                stC2(n - 1)
                stD(n - 1)
            if 0 <= n - LAG < N:
                stE2(n - LAG)
            if side is not None:
                side(n, N + LAG)

    def wout_phase(tg):
        blks = [WB.take() for _ in range(8)]
        for i in range(8):
            tl = tg * 8 + i
            tc = i // 4
            for half in range(2):
                b = palloc()
                P.mm([k for (_, k) in blks] + [('mix', c, tc) for c in range(8)], psk(b),
                     [dict(out=PS[:, b, :], lhsT=MX[:, c, i * 128:(i + 1) * 128],
                           rhs=blks[c][0][:, half * CH:(half + 1) * CH], start=(c == 0), stop=(c == 7))
                      for c in range(8)])
                P.op('dve', psk(b) + [('X', tl)], [('X', tl)],
                     lambda e: e.tensor_tensor(out=X[:, tl, half * CH:(half + 1) * CH],
                                               in0=X[:, tl, half * CH:(half + 1) * CH], in1=PS[:, b, :],
                                               op=ALU.add))
        WB.release(8)

    def ffn_phase(l, tg, store_seq=None):
        for pi, (f0, f1) in enumerate(FPASS):
            nb = f1 - f0
            for s_ in range(nb):
                w, wk = WA.take()
                for tc in range(2):
                    bg = palloc()
                    fm_matmul(bg, w, wk, 0, tc)
                    bu = palloc()
                    fm_matmul(bu, w, wk, 128, tc)
                    sg, sgk = talloc()
                    P.op('act', psk(bg), [sgk], lambda e: e.activation(out=sg[:, 0:CH], in_=PS[:, bg, :], func=AF.Silu))
                    P.op('dve', psk(bu) + [sgk], [('mix', s_, tc)],
                         lambda e: e.tensor_tensor(out=MX[:, s_, tc * CH:(tc + 1) * CH], in0=PS[:, bu, :],
                                                   in1=sg[:, 0:CH], op=ALU.mult))
                WA.release(1)
            blks = [WB.take() for _ in range(nb)]
            for i in range(8):
                tl = tg * 8 + i
                tc = i // 4
                for half in range(2):
                    b = palloc()
                    P.mm([k for (_, k) in blks] + [('mix', s_, tc) for s_ in range(nb)], psk(b),
                         [dict(out=PS[:, b, :], lhsT=MX[:, s_, i * 128:(i + 1) * 128],
                               rhs=blks[s_][0][:, half * CH:(half + 1) * CH], start=(s_ == 0), stop=(s_ == nb - 1))
                          for s_ in range(nb)])
                    P.op('dve', psk(b) + [('X', tl)], [('X', tl)],
                         lambda e: e.tensor_tensor(out=X[:, tl, half * CH:(half + 1) * CH],
                                                   in0=X[:, tl, half * CH:(half + 1) * CH], in1=PS[:, b, :],
                                                   op=ALU.add))
                if store_seq is not None and pi == len(FPASS) - 1:
                    P.dma('sp', [(out_d[store_seq, tl * 128:(tl + 1) * 128, :], X[:, tl, :])],
                          [('X', tl)], [('out', store_seq, tl)], f'st{tl}')
            WB.release(nb)

    epsb = sb("epsb", [128, 2], F32)
    P.op('dve', [], ['epsb'], lambda e: e.memset(epsb[:, 0:1], 1e-6))
    P.op('dve', ['epsb'], ['epsb'], lambda e: e.memset(epsb[:, 1:2], 1e-5))
    epsm6 = epsb[:, 0:1]
    epsm5 = epsb[:, 1:2]
    P._wait('act', [P.res['epsb'][0]])

    for s in range(n_seq):
        for tl in range(16):
            P.dma('sp', [(X[:, tl, :], x_d[s, tl * 128:(tl + 1) * 128, :])], [], [('X', tl)], f'xl{tl}')
        for l in range(L):
            P.dma('pool', [(smw, smw_d[l])], [], ['smw'], 'smw')
            for tg in range(NTG):
                tiles = list(range(tg * 8, tg * 8 + 8))
                norm_phase(l, 0, tiles)
                held = None
                sidest['gen'] = None
                for blk in BLK_ORDER:
                    w, wk = WA.take()
                    if blk < 4:
                        qk_block(l, blk, w, wk, tg)
                        WA.release(1)
                    elif blk < 6:
                        v_block(blk, w, wk, tg)
                        WA.release(1)
                    elif blk == 6:
                        pool_block(l, w, wk, tg)
                        WA.release(1)
                    elif blk == 7:
                        held = (w, wk)
                    else:
                        conv_glu(l, held[0], held[1], w, wk, tg)
                        WA.release(2)
                        if SIDE_CONV:
                            sidest['gen'] = conv_side(l, tg)
                            sidest['left'] = 2 * (2 * (CONVK + 1) + 18)
                if not SIDE_CONV:
                    for _ in conv_side(l, tg):
                        pass

                def side(n, total):
                    k = min(2, -(-sidest['left'] // max(1, total - n)))
                    side_pull(k)
                attention(tg, side=side if SIDE_CONV else None)
                side_pull(10 ** 6)
                wout_phase(tg)
                norm_phase(l, 1, tiles)
                ffn_phase(l, tg, store_seq=(s if l == L - 1 else None))
    outk = [('out', s, tl) for s in range(n_seq) for tl in range(16)]
    P.final_wait('sp', outk)
    return nc, P


def host_prep(inputs, n_layers=L_FULL):
    L = n_layers
    f = np.float32
    w_in = np.asarray(inputs["w_in"], f)[:L]
    win = np.ascontiguousarray(w_in.reshape(L, 8, 128, NWIN, 256).transpose(0, 3, 2, 1, 4))
    wout = np.ascontiguousarray(np.asarray(inputs["w_out"], f)[:L].reshape(L, 8, 128, 1024))
    gu = np.asarray(inputs["ffn_w_gu"], f)[:L]
    g = gu[:, :, :HID].reshape(L, 8, 128, NF, 128)
    u = gu[:, :, HID:].reshape(L, 8, 128, NF, 128)
    wgu = np.ascontiguousarray(np.concatenate([g, u], axis=-1).transpose(0, 3, 2, 1, 4))
    wdn = np.ascontiguousarray(np.asarray(inputs["ffn_w_down"], f)[:L].reshape(L, NF, 128, 1024))
    smw = np.zeros((L, 128, 768), f)
    pw = np.asarray(inputs["pool_w"], f)[:L]
    for fc in range(2):
        for half in range(2):
            gi = fc * 2 + half
            smw[:, half * 64:(half + 1) * 64, fc * 128 + half * 64: fc * 128 + (half + 1) * 64] = pw[:, gi]
    cpw = np.asarray(inputs["conv_pw"], f)[:L]
    smw[:, :, 256:768] = cpw.reshape(L, 2, 128, 256).transpose(0, 2, 1, 3).reshape(L, 128, 512)
    cols = np.zeros((128, L, NCOL), f)
    cols[:, :, C_GQ] = np.tile(np.asarray(inputs["sb_q_g"], f)[:L], (1, 2)).T
    cols[:, :, C_GK] = np.tile(np.asarray(inputs["sb_k_g"], f)[:L], (1, 2)).T

    def pc(a):
        return np.asarray(a, f)[:L].reshape(L, 2, 128).transpose(2, 0, 1)
    cols[:, :, C_PS:C_PS + 2] = pc(inputs["pool_scale"])
    cols[:, :, C_CB:C_CB + 2] = pc(inputs["conv_b"])
    cols[:, :, C_LG:C_LG + 2] = pc(inputs["conv_ln_g"])
    cols[:, :, C_LB:C_LB + 2] = pc(inputs["conv_ln_b"])
    cw = np.asarray(inputs["conv_w"], f)[:L]
    cols[:, :, C_CW:C_CW + 2 * CONVK] = cw.reshape(L, CONVK, 2, 128).transpose(3, 0, 2, 1).reshape(128, L, 2 * CONVK)
    gains = np.ascontiguousarray(np.stack([np.asarray(inputs["norm_mix_g"], f)[:L],
                                           np.asarray(inputs["norm_ffn_g"], f)[:L]], axis=1))
    cbf = np.zeros((128, K_END), f)
    idx = np.arange(128)
    cbf[:, K_ID:K_ID + 128] = np.eye(128, dtype=f)
    cbf[:, K_TRI:K_TRI + 128] = (idx[:, None] >= idx[None, :]).astype(f) * -1.0
    cbf[:, K_ONE:K_ONE + 128] = -1.0
    cbf[:, K_NEG:K_NEG + 128] = np.where(idx[:, None] >= idx[None, :], -30000.0, 0.0)
    cf = np.zeros((128, KF_END), f)
    cf[:, KF_B64:KF_B64 + 128] = ((idx[:, None] // 64) == (idx[None, :] // 64)).astype(f) / 64.0
    cf[:, KF_O256:KF_O256 + 128] = 1.0 / 256.0
    for fc in range(2):
        for half in range(2):
            wdw = 2 << (fc * 2 + half)
            t = np.arange(16)
            cf[half * 64:(half + 1) * 64, KF_FIX + fc * 16:KF_FIX + fc * 16 + 16] = wdw / np.minimum(t + 1.0, wdw)
    return dict(win=win, wout=wout, wgu=wgu, wdn=wdn, smw=smw, cols=cols, gains=gains, cbf=cbf, cf=cf)


_CACHE = {}


def kernel(**inputs):
    x = np.asarray(inputs["x"], np.float32)
    shared = host_prep(inputs)
    if 'nc' not in _CACHE:
        _CACHE['nc'] = build_program()[0]
    nc = _CACHE['nc']
    in_maps = []
    for c in range(NCORES):
        m = dict(shared)
        m["x"] = np.ascontiguousarray(x[c * NSEQ_CORE:(c + 1) * NSEQ_CORE])
        in_maps.append(m)
    res = run_bass_kernel_spmd(nc, in_maps, core_ids=list(range(NCORES)))
    return np.concatenate([r["out"] for r in res.results], axis=0).astype(np.float32)
```

```python
import numpy as np
import concourse.bass as bass
import concourse.mybir as mybir
from concourse.bass_utils import run_bass_kernel_spmd

F32 = mybir.dt.float32
BF16 = mybir.dt.bfloat16
AF = mybir.ActivationFunctionType
ALU = mybir.AluOpType

D = 1024
T = 2048
L_FULL = 4
NSEQ_CORE = 4
NCORES = 8
TG = 1024
NTG = T // TG
CH = 512
HID = 2816
NF = HID // 128
IN_COLS = 2304
NWIN = IN_COLS // 256
CONVK = 31
FPASS = [(0, 8), (8, 15), (15, 22)]
C_GQ, C_GK, C_PS, C_CB, C_LG, C_LB, C_CW = 0, 1, 2, 4, 6, 8, 10
NCOL = 10 + 2 * CONVK
K_ID, K_TRI, K_ONE, K_NEG, K_B64, K_END = 0, 128, 256, 384, 512, 640
KF_B64, KF_O256, KF_FIX, KF_END = 0, 128, 256, 288
TMPW = 544
CONV_ENG = ['dve', 'dve']
NWARM = 20
NFILL = 2
NORM_HOOK = True
SIDE_CONV = True
A_ENG = 'dve'
BLK_ORDER = [7, 8, 6, 0, 1, 2, 3, 4, 5]


class Prog:
    def __init__(self, nc):
        self.nc = nc
        self.E = {'pe': nc.tensor, 'act': nc.scalar, 'dve': nc.vector, 'pool': nc.gpsimd, 'sp': nc.sync}
        self.esem = {}
        self.ecnt = {}
        for e in ['pe', 'act', 'dve', 'pool']:
            self.esem[e] = nc.alloc_semaphore('sem_' + e)
            self.ecnt[e] = 0
        self.waited = {}
        self.res = {}
        self.dsem = {}
        self.nins = 0

    @staticmethod
    def _flat(keys):
        out = []
        for k in keys:
            if isinstance(k, list):
                out.extend(Prog._flat(k))
            else:
                out.append(k)
        return out

    def _deps(self, reads, writes):
        reads = self._flat(reads)
        writes = self._flat(writes)
        toks = []
        for k in reads:
            r = self.res.get(k)
            if r is not None and r[0] is not None:
                toks.append(r[0])
        for k in writes:
            r = self.res.get(k)
            if r is not None:
                if r[0] is not None:
                    toks.append(r[0])
                toks.extend(r[1].values())
        return toks

    def _commit(self, tok, reads, writes):
        reads = self._flat(reads)
        writes = self._flat(writes)
        for k in reads:
            r = self.res.get(k)
            if r is None:
                r = [None, {}]
                self.res[k] = r
            old = r[1].get(tok[0])
            if old is None or old[2] < tok[2]:
                r[1][tok[0]] = tok
        for k in writes:
            self.res[k] = [tok, {}]

    def _wait(self, eng, toks):
        best = {}
        for (name, sem, val) in toks:
            if eng == 'pe' and name == 'sem_pe':
                continue
            b = best.get(name)
            if b is None or b[1] < val:
                best[name] = (sem, val)
        for name, (sem, val) in best.items():
            if self.waited.get((eng, name), 0) >= val:
                continue
            self.E[eng].wait_ge(sem, val)
            self.waited[(eng, name)] = val
            self.nins += 1

    def op(self, eng, reads, writes, fn):
        self._wait(eng, self._deps(reads, writes))
        ins = fn(self.E[eng])
        self.ecnt[eng] += 1
        ins.then_inc(self.esem[eng], 1)
        tok = ('sem_' + eng, self.esem[eng], self.ecnt[eng])
        self._commit(tok, reads, writes)
        self.nins += 1
        return tok

    def mm(self, reads, writes, mms):
        self._wait('pe', self._deps(reads, writes))
        ins = None
        for m in mms:
            ins = self.nc.tensor.matmul(**m)
            self.nins += 1
        self.ecnt['pe'] += 1
        ins.then_inc(self.esem['pe'], 1)
        tok = ('sem_pe', self.esem['pe'], self.ecnt['pe'])
        self._commit(tok, reads, writes)
        return tok

    def mm_batch(self, groups):
        toks = []
        for (reads, writes, mms) in groups:
            toks += self._deps(reads, writes)
        self._wait('pe', toks)
        for (reads, writes, mms) in groups:
            ins = None
            for m in mms:
                ins = self.nc.tensor.matmul(**m)
                self.nins += 1
            self.ecnt['pe'] += 1
            ins.then_inc(self.esem['pe'], 1)
            tok = ('sem_pe', self.esem['pe'], self.ecnt['pe'])
            self._commit(tok, reads, writes)

    def pe_ops(self, reads, writes, fns):
        self._wait('pe', self._deps(reads, writes))
        ins = None
        for f in fns:
            ins = f(self.nc.tensor)
            self.nins += 1
        self.ecnt['pe'] += 1
        ins.then_inc(self.esem['pe'], 1)
        tok = ('sem_pe', self.esem['pe'], self.ecnt['pe'])
        self._commit(tok, reads, writes)
        return tok

    def dma(self, q, pairs, reads, writes, semkey):
        self._wait(q, self._deps(reads, writes))
        d = self.dsem.get(semkey)
        if d is None:
            d = [self.nc.alloc_semaphore('dma_' + semkey), 0]
            self.dsem[semkey] = d
        for (out, in_) in pairs:
            self.E[q].dma_start(out=out, in_=in_).then_inc(d[0], 16)
            d[1] += 16
            self.nins += 1
        tok = ('dma_' + semkey, d[0], d[1])
        self._commit(tok, reads, writes)
        return tok

    def final_wait(self, eng, keys):
        toks = []
        for k in self._flat(keys):
            r = self.res.get(k)
            if r is not None:
                if r[0] is not None:
                    toks.append(r[0])
                toks.extend(r[1].values())
        self._wait(eng, toks)


class WStream:
    def __init__(self, prog, name, nslots, shape, plan, queue):
        self.p = prog
        self.name = name
        self.n = nslots
        self.plan = plan
        self.queue = queue
        self.slots = [prog.nc.alloc_sbuf_tensor(f'{name}_s{i}', shape, BF16).ap() for i in range(nslots)]
        self.issued = 0
        self.taken = 0
        self.released = 0

    def key(self, i):
        return (self.name, i % self.n)

    def prefetch(self):
        while self.issued < len(self.plan) and self.issued < self.released + self.n:
            i = self.issued
            self.p.dma(self.queue, [(self.slots[i % self.n], self.plan[i])], [], [self.key(i)],
                       f'{self.name}{i % self.n}')
            self.issued += 1

    def take(self):
        i = self.taken
        assert i < self.issued, (self.name, i, self.issued)
        self.taken += 1
        return self.slots[i % self.n], self.key(i)

    def release(self, k=1):
        self.released += k
        self.prefetch()


def build_program(n_layers=L_FULL, n_seq=NSEQ_CORE, wq='pool'):
    nc = bass.Bass("TRN2", target_bir_lowering=False)
    P = Prog(nc)
    L = n_layers

    x_d = nc.dram_tensor("x", [n_seq, T, D], F32, kind="ExternalInput").ap()
    win_d = nc.dram_tensor("win", [L, NWIN, 128, 8, 256], F32, kind="ExternalInput").ap()
    wout_d = nc.dram_tensor("wout", [L, 8, 128, 1024], F32, kind="ExternalInput").ap()
    wgu_d = nc.dram_tensor("wgu", [L, NF, 128, 8, 256], F32, kind="ExternalInput").ap()
    wdn_d = nc.dram_tensor("wdn", [L, NF, 128, 1024], F32, kind="ExternalInput").ap()
    smw_d = nc.dram_tensor("smw", [L, 128, 768], F32, kind="ExternalInput").ap()
    cols_d = nc.dram_tensor("cols", [128, L, NCOL], F32, kind="ExternalInput").ap()
    gains_d = nc.dram_tensor("gains", [L, 2, D], F32, kind="ExternalInput").ap()
    cbf_d = nc.dram_tensor("cbf", [128, K_END], F32, kind="ExternalInput").ap()
    cf_d = nc.dram_tensor("cf", [128, KF_END], F32, kind="ExternalInput").ap()
    out_d = nc.dram_tensor("out", [n_seq, T, D], F32, kind="ExternalOutput").ap()

    def sb(name, shape, dt):
        return nc.alloc_sbuf_tensor(name, shape, dt).ap()

    X = sb("X", [128, 16, D], F32)
    hT = sb("hT", [128, 8, TG], BF16)
    kT = sb("kT", [128, 4, T], BF16)
    V = sb("V", [128, 16, 512], BF16)
    qT = sb("qT", [128, 4, TG], BF16)
    MX = sb("MX", [128, 8, TG], BF16)
    NTMP = 11
    NROT = 8
    tmpool = sb("tmpool", [128, NTMP, TMPW], F32)
    tmp = [tmpool[:, i, :] for i in range(NTMP)]
    up = sb("up", [128, 2, 16 + CH], F32)
    hc2 = sb("hc2", [128, 2, 2, 30 + CH], F32)
    gain = tmpool[:, 9:11, 0:512]
    KGAIN = [('tmp', 9, 0), ('tmp', 9, 1), ('tmp', 10, 0), ('tmp', 10, 1)]
    cbf = sb("cbf_sb", [128, K_END], BF16)
    z512 = sb("z512", [128, CH], BF16)
    cf = sb("cf_sb", [128, KF_END], F32)
    cols = sb("cols_sb", [128, L, NCOL], F32)
    gq8 = sb("gq8", [128, L], F32)
    smw = sb("smw_sb", [128, 768], BF16)
    ss = sb("ss", [128, 8], F32)
    rs = sb("rs", [128, 8], F32)
    rs2 = sb("rs2", [128, 8], F32)
    PS = nc.alloc_psum_tensor("PS", [128, 8, 512], F32).ap()

    ident = cbf[:, K_ID:K_ID + 128]
    triI = cbf[:, K_TRI:K_TRI + 128]
    ones = cbf[:, K_ONE:K_ONE + 128]
    negm = cbf[:, K_NEG:K_NEG + 128]
    b64 = cbf[:, K_B64:K_B64 + 128]
    o256 = cf[:, KF_O256:KF_O256 + 128]

    def psk(b):
        return [('ps', b, 0), ('ps', b, 1)]

    tstate = {'i': 0}

    def talloc():
        i = tstate['i'] % NROT
        tstate['i'] += 1
        return tmp[i], [('tmp', i, 0), ('tmp', i, 1)]

    pstate = {'i': 0}

    def palloc(banks=(0, 1, 2, 3, 4, 5, 7)):
        b = banks[pstate['i'] % len(banks)]
        pstate['i'] += 1
        return b

    planA, planB = [], []
    for s in range(n_seq):
        for l in range(L):
            for tg in range(NTG):
                for j in BLK_ORDER:
                    planA.append(win_d[l, j])
                for c in range(8):
                    planB.append(wout_d[l, c])
                for (f0, f1) in FPASS:
                    for f in range(f0, f1):
                        planA.append(wgu_d[l, f])
                    for f in range(f0, f1):
                        planB.append(wdn_d[l, f])
    WA = WStream(P, 'wa', 3, [128, 8, 256], planA, wq)
    WB = WStream(P, 'wb', 8, [128, 1024], planB, wq)

    P.dma('pool', [(cbf, cbf_d)], [], ['cbf'], 'c0')
    P.dma('sp', [(cf, cf_d)], [], ['cf'], 'c1')
    P.dma('sp', [(cols, cols_d)], [], ['cols'], 'c2')
    P.op('pool', [], ['z512'], lambda e: e.memset(z512, 0.0))
    P.op('act', ['cols'], ['gq8'], lambda e: e.mul(out=gq8, in_=cols[:, :, C_GQ], mul=0.125))
    WA.prefetch()
    WB.prefetch()

    sidest = {'gen': None, 'left': 0}

    def side_pull(k):
        g = sidest['gen']
        if g is None:
            return
        for _ in range(k):
            try:
                next(g)
                sidest['left'] -= 1
            except StopIteration:
                sidest['gen'] = None
                sidest['left'] = 0
                return

    def norm_phase(l, which, tiles):
        P.dma('sp', [(gain, gains_d[l, which].partition_broadcast(128).rearrange("p (a b) -> p a b", a=2))],
              [], [KGAIN], 'gain')
        for i, tl in enumerate(tiles):
            jk, jkk = talloc()
            P.op('act', [('X', tl)], [jkk, ('ss', i)],
                 lambda e: e.activation(out=jk[:, 0:512].bitcast(BF16), in_=X[:, tl, :], func=AF.Square,
                                        accum_out=ss[:, i:i + 1]))
        ssk = [('ss', i) for i in range(8)]
        P.op('act', ssk, ['rs'], lambda e: e.activation(out=rs, in_=ss, func=AF.Ln, scale=1.0 / D, bias=epsm6))
        P.op('act', ['rs'], ['rs2'], lambda e: e.activation(out=rs2, in_=rs, func=AF.Exp, scale=-0.5))
        for i, tl in enumerate(tiles):
            xt_, xnk = talloc()
            xb = xt_[:, 0:512].bitcast(BF16)
            P.op('dve', [('X', tl), 'rs2', KGAIN], [xnk],
                 lambda e: e.scalar_tensor_tensor(out=xb.rearrange("p (a b) -> p a b", a=2),
                                                  in0=X[:, tl, :].rearrange("p (a b) -> p a b", a=2),
                                                  scalar=rs2[:, i:i + 1], in1=gain,
                                                  op0=ALU.mult, op1=ALU.mult))
            b = palloc()
            pb = PS[:, b, :].bitcast(BF16)
            P.pe_ops([xnk, 'cbf'], psk(b),
                     [(lambda e, c=c: e.transpose(out=pb[:, c * 128:(c + 1) * 128],
                                                  in_=xb[:, c * 128:(c + 1) * 128], identity=ident))
                      for c in range(8)])
            P.op('act', psk(b), [('hT', i)],
                 lambda e: e.copy(out=hT[:, :, i * 128:(i + 1) * 128],
                                  in_=pb.rearrange("p (c t) -> p c t", c=8)))

    def fm_matmul(b, w, wkey, col0, tc):
        P.mm([wkey] + [('hT', tc * 4 + i) for i in range(4)], psk(b),
             [dict(out=PS[:, b, :], lhsT=w[:, c, col0:col0 + 128], rhs=hT[:, c, tc * CH:(tc + 1) * CH],
                   start=(c == 0), stop=(c == 7)) for c in range(8)])

    def qk_block(l, blk, w, wkey, tg):
        isq = blk < 2
        for f2 in range(2):
            fc = (blk % 2) * 2 + f2
            gcol = gq8[:, l:l + 1] if isq else cols[:, l, C_GK:C_GK + 1]
            for tc in range(2):
                b = palloc()
                fm_matmul(b, w, wkey, f2 * 128, tc)
                sq, sqk = talloc()
                sqb = sq[:, 0:CH // 2].bitcast(BF16)
                P.op('act', psk(b), [sqk], lambda e: e.activation(out=sqb, in_=PS[:, b, :], func=AF.Square))
                b2 = palloc()
                P.mm([sqk, 'cbf'], psk(b2), [dict(out=PS[:, b2, :], lhsT=b64, rhs=sqb, start=True, stop=True)])
                sd, sdk = talloc()
                P.op('act', psk(b2), [sdk],
                     lambda e: e.activation(out=sd[:, 0:CH], in_=PS[:, b2, :], func=AF.Ln, bias=epsm6))
                ri, rik = sd, sdk
                P.op('act', [sdk], [sdk],
                     lambda e: e.activation(out=sd[:, 0:CH], in_=sd[:, 0:CH], func=AF.Exp, scale=-0.5))
                if isq:
                    dst = qT[:, fc, tc * CH:(tc + 1) * CH]
                    dk = ('qT', fc, tc)
                else:
                    g0 = tg * TG + tc * CH
                    dst = kT[:, fc, g0:g0 + CH]
                    dk = ('kT', fc, tg * 2 + tc)
                P.op('dve', psk(b) + [rik, 'cols', 'gq8'], [dk],
                     lambda e: e.scalar_tensor_tensor(out=dst, in0=PS[:, b, :], scalar=gcol, in1=ri[:, 0:CH],
                                                      op0=ALU.mult, op1=ALU.mult))
                side_pull(2)

    def v_block(blk, w, wkey, tg):
        vb = blk - 4
        for i in range(8):
            tl = tg * 8 + i
            b = palloc()
            P.mm([wkey, ('hT', i)], psk(b),
                 [dict(out=PS[:, b, 0:256], lhsT=hT[:, c, i * 128:(i + 1) * 128], rhs=w[:, c, :],
                       start=(c == 0), stop=(c == 7)) for c in range(8)])
            P.op('act', psk(b), [('V', tl, vb)],
                 lambda e: e.copy(out=V[:, tl, vb * 256:(vb + 1) * 256], in_=PS[:, b, 0:256]))
            side_pull(2)

    def pool_block(l, w, wkey, tg):
        pw = smw[:, 0:256].rearrange("p (f d) -> p f d", f=2)
        for tc in range(2):
            gc = tg * 2 + tc
            for fc in range(2):
                if gc == 0:
                    P.op('dve', [], [('up', fc)], lambda e: e.memset(up[:, fc, 0:16], 0.0))
                else:
                    P.op('dve', [], [('up', fc)],
                         lambda e: e.tensor_copy(out=up[:, fc, 0:16], in_=up[:, fc, CH:CH + 16]))
                b = palloc()
                fm_matmul(b, w, wkey, fc * 128, tc)
                P.op('act', psk(b), [('up', fc)], lambda e: e.copy(out=up[:, fc, 16:16 + CH], in_=PS[:, b, :]))
                u = up[:, fc, :]
                W_ = 16 + CH
                lv = []
                prev, prevk = u, ('up', fc)
                nlev = 2 if fc == 0 else 4
                for li in range(nlev):
                    sh = 1 << li
                    lo = 2 * sh - 1
                    t_, tk = talloc()
                    P.op('dve', [prevk], [tk],
                         lambda e: e.tensor_tensor(out=t_[:, lo:W_], in0=prev[:, lo:W_], in1=prev[:, lo - sh:W_ - sh],
                                                   op=ALU.add))
                    lv.append((t_, tk))
                    prev, prevk = t_, tk
                pl, plk = talloc()
                plb = pl[:, 0:CH // 2].bitcast(BF16)
                for half in range(2):
                    g = fc * 2 + half
                    wdw = 2 << g
                    s_, sk = lv[g]
                    ps_ = slice(half * 64, (half + 1) * 64)
                    if gc == 0:
                        P.op('dve', [sk, 'cf'], [sk],
                             lambda e: e.tensor_tensor(out=s_[ps_, 16:32], in0=s_[ps_, 16:32],
                                                       in1=cf[ps_, KF_FIX + fc * 16:KF_FIX + fc * 16 + 16],
                                                       op=ALU.mult))
                    P.op('dve', [sk, ('up', fc)], [plk],
                         lambda e: e.scalar_tensor_tensor(out=plb[ps_, :], in0=s_[ps_, 16:16 + CH],
                                                          scalar=1.0 / wdw, in1=u[ps_, 16:16 + CH],
                                                          op0=ALU.mult, op1=ALU.subtract))
                b2 = palloc()
                P.mm([plk, 'smw'], psk(b2), [dict(out=PS[:, b2, :], lhsT=pw[:, fc, :], rhs=plb, start=True, stop=True)])
                P.op('act', psk(b2) + ['cols'], [('mix', 4 + fc, tc)],
                     lambda e: e.activation(out=MX[:, 4 + fc, tc * CH:(tc + 1) * CH], in_=PS[:, b2, :], func=AF.Copy,
                                            scale=cols[:, l, C_PS + fc:C_PS + fc + 1]))

    def conv_glu(l, wa_, wak, wg_, wgk, tg):
        for tc in range(2):
            gc = tg * 2 + tc
            for fc in range(2):
                dst = hc2[:, tc, fc, :]
                if gc == 0:
                    P.op('dve', [], [('hc', tc, fc)], lambda e: e.memset(dst[:, 0:30], 0.0))
                else:
                    P.op('dve', [('hc', 1 - tc, fc)], [('hc', tc, fc)],
                         lambda e: e.tensor_copy(out=dst[:, 0:30], in_=hc2[:, 1 - tc, fc, CH:CH + 30]))
                ba = palloc()
                fm_matmul(ba, wa_, wak, fc * 128, tc)
                bg = palloc()
                fm_matmul(bg, wg_, wgk, fc * 128, tc)
                sg, sgk = talloc()
                P.op('act', psk(bg), [sgk], lambda e: e.activation(out=sg[:, 0:CH], in_=PS[:, bg, :], func=AF.Sigmoid))
                P.op('dve', psk(ba) + [sgk], [('hc', tc, fc)],
                     lambda e: e.tensor_tensor(out=dst[:, 30:30 + CH], in0=PS[:, ba, :], in1=sg[:, 0:CH],
                                               op=ALU.mult))

    def conv_side(l, tg):
        cpw = smw[:, 256:768].rearrange("p (f d) -> p f d", f=2)
        T8, T9, T10 = tmp[8], tmp[9], tmp[10]
        K8 = [('tmp', 8, 0), ('tmp', 8, 1)]
        K9 = [('tmp', 9, 0), ('tmp', 9, 1)]
        K10 = [('tmp', 10, 0), ('tmp', 10, 1)]
        B6 = 6
        for tc in range(2):
            accs = []
            for fc in range(2):
                src = hc2[:, tc, fc, :]
                hk = ('hc', tc, fc)
                cw = cols[:, l, C_CW + fc * CONVK:C_CW + (fc + 1) * CONVK]
                P.op('dve', [hk, 'cols'], [K8],
                     lambda e: e.tensor_scalar(out=T8[:, 0:CH], in0=src[:, 0:CH], scalar1=cw[:, 0:1],
                                               scalar2=cols[:, l, C_CB + fc:C_CB + fc + 1],
                                               op0=ALU.mult, op1=ALU.add))
                yield
                for k in range(1, CONVK):
                    P.op('dve', [hk, 'cols', K8], [K8],
                         lambda e: e.scalar_tensor_tensor(out=T8[:, 0:CH], in0=src[:, k:k + CH],
                                                          scalar=cw[:, k:k + 1], in1=T8[:, 0:CH],
                                                          op0=ALU.mult, op1=ALU.add))
                    yield
                P.op('dve', [K8], [hk], lambda e: e.tensor_copy(out=src[:, 0:CH], in_=T8[:, 0:CH]))
                yield
                accs.append((src[:, 0:CH], hk))
            aks = [accs[0][1], accs[1][1]]
            P.mm(aks + ['cf'], psk(B6),
                 [dict(out=PS[:, B6, :], lhsT=o256, rhs=accs[fc][0], start=(fc == 0), stop=(fc == 1))
                  for fc in range(2)])
            yield
            P.op('dve', psk(B6), [K9], lambda e: e.tensor_copy(out=T9[:, 0:CH], in_=PS[:, B6, :]))
            yield
            for fc in range(2):
                a_, ak = accs[fc]
                P.op('dve', [ak, K9], [ak],
                     lambda e: e.tensor_tensor(out=a_, in0=a_, in1=T9[:, 0:CH], op=ALU.subtract))
                yield
            for fc in range(2):
                a_, ak = accs[fc]
                P.op('dve', [ak], [K8], lambda e: e.tensor_tensor(out=T8[:, 0:CH], in0=a_, in1=a_, op=ALU.mult))
                yield
                P.mm([K8, 'cf'], psk(B6),
                     [dict(out=PS[:, B6, :], lhsT=o256, rhs=T8[:, 0:CH], start=(fc == 0), stop=(fc == 1),
                           skip_group_check=True)])
                yield
            P.op('act', psk(B6), [K10],
                 lambda e: e.activation(out=T10[:, 0:CH], in_=PS[:, B6, :], func=AF.Ln, bias=epsm5))
            yield
            P.op('act', [K10], [K10],
                 lambda e: e.activation(out=T10[:, 0:CH], in_=T10[:, 0:CH], func=AF.Exp, scale=-0.5))
            yield
            ysl = []
            for fc in range(2):
                a_, ak = accs[fc]
                P.op('dve', [ak, K10], [ak],
                     lambda e: e.tensor_tensor(out=a_, in0=a_, in1=T10[:, 0:CH], op=ALU.mult))
                yield
                ysb = T8[:, fc * 272:fc * 272 + 256].bitcast(BF16)
                ysk = ('tmp', 8, fc)
                P.op('act', [ak, 'cols'], [ysk],
                     lambda e: e.activation(out=ysb, in_=a_, func=AF.Silu,
                                            scale=cols[:, l, C_LG + fc:C_LG + fc + 1],
                                            bias=cols[:, l, C_LB + fc:C_LB + fc + 1]))
                yield
                ysl.append((ysb, ysk))
            for fo in range(2):
                P.mm([ysl[0][1], ysl[1][1], 'smw'], psk(B6),
                     [dict(out=PS[:, B6, :], lhsT=cpw[:, fc, fo * 128:(fo + 1) * 128], rhs=ysl[fc][0],
                           start=(fc == 0), stop=(fc == 1)) for fc in range(2)])
                yield
                P.op('dve', psk(B6), [('mix', 6 + fo, tc)],
                     lambda e: e.tensor_copy(out=MX[:, 6 + fo, tc * CH:(tc + 1) * CH], in_=PS[:, B6, :]))
                yield

    def attention(tg, side=None):
        its = []
        for tc in range(2):
            gq = tg * 2 + tc
            i0 = 4 * gq
            for pair in range(4):
                ob = 4 + ((gq * 4 + pair) % 2)
                for kb in range(i0 + 3, -1, -1):
                    its.append(dict(tc=tc, gq=gq, i0=i0, pair=pair, kb=kb,
                                    first=(kb == i0 + 3), last=(kb == 0), ob=ob))
        N = len(its)

        def tk(i):
            return [('tmp', i, 0), ('tmp', i, 1)]

        def bfpair(i):
            return tmpool[:, i, :].bitcast(BF16).rearrange("p (h c) -> p h c", h=2)[:, :, 0:CH]
        Ef = tmpool[:, 0:2, 0:CH]
        KE = tk(0) + tk(1)
        SPb = [bfpair(2), bfpair(3)]
        KSP = [tk(2), tk(3)]
        Wb = [bfpair(4), bfpair(5), bfpair(6)]
        KW = [tk(4), tk(5), tk(6)]
        Ab = bfpair(7)
        KA = tk(7)
        ZB, SB = 0, 2

        def geo(d):
            c0 = max(0, d['kb'] - d['i0']) * 128
            diag = d['kb'] >= d['i0']
            return c0, diag

        def qk_pair(d, bank, start):
            c0, diag = geo(d)
            fc = d['pair']
            q0 = d['tc'] * CH
            mms = []
            for hp in range(2):
                pr = slice(hp * 64, (hp + 1) * 64)
                mms.append(dict(out=PS[:, bank + hp, c0:CH], lhsT=kT[pr, fc, d['kb'] * 128:(d['kb'] + 1) * 128],
                                rhs=qT[pr, fc, q0 + c0:q0 + CH], start=start, stop=not diag))
            if diag:
                for hp in range(2):
                    mms.append(dict(out=PS[:, bank + hp, c0:c0 + 128], lhsT=ident, rhs=negm, start=False, stop=True))
            return mms

        def stA(n):
            d = its[n]
            fc = d['pair']
            pegroups.append(([('kT', fc, d['kb'] // 4), ('qT', fc, d['tc']), 'cbf'], psk(ZB) + psk(ZB + 1),
                             qk_pair(d, ZB, True)))
            if d['first']:
                pegroups.append((['cbf', 'z512'], psk(d['ob']),
                                 [dict(out=PS[:, d['ob'], :], lhsT=ones, rhs=z512, start=True, stop=False,
                                       skip_group_check=True)]))

        def stB(n):
            d = its[n]
            c0, diag = geo(d)
            P.op('act', psk(ZB) + psk(ZB + 1), [KE],
                 lambda e: e.activation(out=Ef[:, :, c0:CH], in_=PS[:, ZB:ZB + 2, c0:CH], func=AF.Exp))
            P.op('act', [KE], [KSP[n % 2]],
                 lambda e: e.activation(out=SPb[n % 2][:, :, c0:CH], in_=Ef[:, :, c0:CH], func=AF.Ln, bias=1.0))

        def stC(n):
            d = its[n]
            c0, diag = geo(d)
            fc = d['pair']
            mms = []
            rd = [KSP[n % 2], 'cbf', ('kT', fc, d['kb'] // 4), ('qT', fc, d['tc'])]
            for hp in range(2):
                mms.append(dict(out=PS[:, SB + hp, c0:CH], lhsT=triI, rhs=SPb[n % 2][:, hp, c0:CH],
                                start=True, stop=False))
                if not d['first']:
                    mms.append(dict(out=PS[:, SB + hp, c0:CH], lhsT=ones, rhs=Ab[:, hp, c0:CH],
                                    start=False, stop=False))
            if not d['first']:
                rd.append(KA)
            mms += qk_pair(d, SB, False)
            pegroups.append((rd, psk(SB) + psk(SB + 1), mms))

        def stC2(n):
            d = its[n]
            c0, diag = geo(d)
            if d['first']:
                P.op(A_ENG, [], [KA], lambda e: e.memset(Ab, 0.0))
            if not d['last']:
                P.op(A_ENG, [KSP[n % 2], KA], [KA],
                     lambda e: e.tensor_tensor(out=Ab[:, :, c0:CH], in0=Ab[:, :, c0:CH],
                                               in1=SPb[n % 2][:, :, c0:CH], op=ALU.add))

        def stD(n):
            d = its[n]
            c0, diag = geo(d)
            P.op('act', psk(SB) + psk(SB + 1), [KW[n % 3]],
                 lambda e: e.activation(out=Wb[n % 3][:, :, c0:CH], in_=PS[:, SB:SB + 2, c0:CH], func=AF.Exp))

        def stE(n):
            d = its[n]
            c0, diag = geo(d)
            fc = d['pair']
            mms = []
            for hp in range(2):
                h = fc * 2 + hp
                pr = slice(hp * 64, (hp + 1) * 64)
                mms.append(dict(out=PS[pr, d['ob'], c0:CH], lhsT=V[:, d['kb'], h * 64:(h + 1) * 64],
                                rhs=Wb[n % 3][:, hp, c0:CH], start=False, stop=d['last'], skip_group_check=True))
            pegroups.append(([KW[n % 3], ('V', d['kb'], fc // 2)], psk(d['ob']), mms))

        def stE2(n):
            d = its[n]
            fc = d['pair']
            if d['last']:
                P.op('act', psk(d['ob']), [('mix', fc, d['tc'])],
                     lambda e: e.copy(out=MX[:, fc, d['tc'] * CH:(d['tc'] + 1) * CH], in_=PS[:, d['ob'], :]))

        pegroups = []
        LAG = 2
        P.mm([('hT', 0), 'cbf'], psk(7),
             [dict(out=PS[:, 7, :], lhsT=ident, rhs=hT[:, w_ % 8, 0:CH], start=True, stop=True)
              for w_ in range(NWARM)])
        for n in range(N + LAG):
            del pegroups[:]
            if n < N:
                stA(n)
                P.mm_batch(pegroups)
                del pegroups[:]
            if 0 <= n - 1 < N:
                stC(n - 1)
            if 0 <= n - LAG < N:
                stE(n - LAG)
            if NFILL:
                pegroups.append(([('hT', 0), 'cbf'], psk(7),
                                 [dict(out=PS[:, 7, :], lhsT=ident, rhs=hT[:, w_ % 8, 0:CH], start=True, stop=True)
                                  for w_ in range(NFILL)]))
            P.mm_batch(pegroups)
            if n < N:
                stB(n)
            if 0 <= n - 1 < N:
                stC2(n - 1)
                stD(n - 1)
            if 0 <= n - LAG < N:
                stE2(n - LAG)
            if side is not None:
                side(n, N + LAG)

    def wout_phase(tg):
        blks = [WB.take() for _ in range(8)]
        for i in range(8):
            tl = tg * 8 + i
            tc = i // 4
            for half in range(2):
                b = palloc()
                P.mm([k for (_, k) in blks] + [('mix', c, tc) for c in range(8)], psk(b),
                     [dict(out=PS[:, b, :], lhsT=MX[:, c, i * 128:(i + 1) * 128],
                           rhs=blks[c][0][:, half * CH:(half + 1) * CH], start=(c == 0), stop=(c == 7))
                      for c in range(8)])
                P.op('dve', psk(b) + [('X', tl)], [('X', tl)],
                     lambda e: e.tensor_tensor(out=X[:, tl, half * CH:(half + 1) * CH],
                                               in0=X[:, tl, half * CH:(half + 1) * CH], in1=PS[:, b, :],
                                               op=ALU.add))
        WB.release(8)

    def ffn_phase(l, tg, store_seq=None, hook=None):
        for pi, (f0, f1) in enumerate(FPASS):
            nb = f1 - f0
            for s_ in range(nb):
                w, wk = WA.take()
                for tc in range(2):
                    bg = palloc()
                    fm_matmul(bg, w, wk, 0, tc)
                    bu = palloc()
                    fm_matmul(bu, w, wk, 128, tc)
                    sg, sgk = talloc()
                    P.op('act', psk(bg), [sgk], lambda e: e.activation(out=sg[:, 0:CH], in_=PS[:, bg, :], func=AF.Silu))
                    P.op('dve', psk(bu) + [sgk], [('mix', s_, tc)],
                         lambda e: e.tensor_tensor(out=MX[:, s_, tc * CH:(tc + 1) * CH], in0=PS[:, bu, :],
                                                   in1=sg[:, 0:CH], op=ALU.mult))
                WA.release(1)
            if hook is not None and pi == len(FPASS) - 1:
                hook()
            blks = [WB.take() for _ in range(nb)]
            for i in range(8):
                tl = tg * 8 + i
                tc = i // 4
                for half in range(2):
                    b = palloc()
                    P.mm([k for (_, k) in blks] + [('mix', s_, tc) for s_ in range(nb)], psk(b),
                         [dict(out=PS[:, b, :], lhsT=MX[:, s_, i * 128:(i + 1) * 128],
                               rhs=blks[s_][0][:, half * CH:(half + 1) * CH], start=(s_ == 0), stop=(s_ == nb - 1))
                          for s_ in range(nb)])
                    P.op('dve', psk(b) + [('X', tl)], [('X', tl)],
                         lambda e: e.tensor_tensor(out=X[:, tl, half * CH:(half + 1) * CH],
                                                   in0=X[:, tl, half * CH:(half + 1) * CH], in1=PS[:, b, :],
                                                   op=ALU.add))
                if store_seq is not None and pi == len(FPASS) - 1:
                    P.dma('sp', [(out_d[store_seq, tl * 128:(tl + 1) * 128, :], X[:, tl, :])],
                          [('X', tl)], [('out', store_seq, tl)], f'st{tl}')
            WB.release(nb)

    epsb = sb("epsb", [128, 2], F32)
    P.op('dve', [], ['epsb'], lambda e: e.memset(epsb[:, 0:1], 1e-6))
    P.op('dve', ['epsb'], ['epsb'], lambda e: e.memset(epsb[:, 1:2], 1e-5))
    epsm6 = epsb[:, 0:1]
    epsm5 = epsb[:, 1:2]
    P._wait('act', [P.res['epsb'][0]])

    normed = {}
    for s in range(n_seq):
        for tl in range(16):
            P.dma('sp', [(X[:, tl, :], x_d[s, tl * 128:(tl + 1) * 128, :])], [], [('X', tl)], f'xl{tl}')
        for l in range(L):
            P.dma('pool', [(smw, smw_d[l])], [], ['smw'], 'smw')
            for tg in range(NTG):
                tiles = list(range(tg * 8, tg * 8 + 8))
                if not normed.pop((s, l, tg), False):
                    norm_phase(l, 0, tiles)
                held = None
                sidest['gen'] = None
                for blk in BLK_ORDER:
                    w, wk = WA.take()
                    if blk < 4:
                        qk_block(l, blk, w, wk, tg)
                        WA.release(1)
                    elif blk < 6:
                        v_block(blk, w, wk, tg)
                        WA.release(1)
                    elif blk == 6:
                        pool_block(l, w, wk, tg)
                        WA.release(1)
                    elif blk == 7:
                        held = (w, wk)
                    else:
                        conv_glu(l, held[0], held[1], w, wk, tg)
                        WA.release(2)
                        if SIDE_CONV:
                            sidest['gen'] = conv_side(l, tg)
                            sidest['left'] = 2 * (2 * (CONVK + 1) + 18)
                if not SIDE_CONV:
                    for _ in conv_side(l, tg):
                        pass

                def side(n, total):
                    k = min(2, -(-sidest['left'] // max(1, total - n)))
                    side_pull(k)
                attention(tg, side=side if SIDE_CONV else None)
                side_pull(10 ** 6)
                wout_phase(tg)
                norm_phase(l, 1, tiles)
                nxt = (s, l, tg + 1) if tg + 1 < NTG else ((s, l + 1, 0) if l + 1 < L else None)
                hook = None
                if nxt is not None and NORM_HOOK:
                    def hook(nxt=nxt):
                        normed[nxt] = True
                        norm_phase(nxt[1], 0, list(range(nxt[2] * 8, nxt[2] * 8 + 8)))
                ffn_phase(l, tg, store_seq=(s if l == L - 1 else None), hook=hook)
    outk = [('out', s, tl) for s in range(n_seq) for tl in range(16)]
    P.final_wait('sp', outk)
    return nc, P


def host_prep(inputs, n_layers=L_FULL):
    L = n_layers
    f = np.float32
    w_in = np.asarray(inputs["w_in"], f)[:L]
    win = np.ascontiguousarray(w_in.reshape(L, 8, 128, NWIN, 256).transpose(0, 3, 2, 1, 4))
    wout = np.ascontiguousarray(np.asarray(inputs["w_out"], f)[:L].reshape(L, 8, 128, 1024))
    gu = np.asarray(inputs["ffn_w_gu"], f)[:L]
    g = gu[:, :, :HID].reshape(L, 8, 128, NF, 128)
    u = gu[:, :, HID:].reshape(L, 8, 128, NF, 128)
    wgu = np.ascontiguousarray(np.concatenate([g, u], axis=-1).transpose(0, 3, 2, 1, 4))
    wdn = np.ascontiguousarray(np.asarray(inputs["ffn_w_down"], f)[:L].reshape(L, NF, 128, 1024))
    smw = np.zeros((L, 128, 768), f)
    pw = np.asarray(inputs["pool_w"], f)[:L]
    for fc in range(2):
        for half in range(2):
            gi = fc * 2 + half
            smw[:, half * 64:(half + 1) * 64, fc * 128 + half * 64: fc * 128 + (half + 1) * 64] = pw[:, gi]
    cpw = np.asarray(inputs["conv_pw"], f)[:L]
    smw[:, :, 256:768] = cpw.reshape(L, 2, 128, 256).transpose(0, 2, 1, 3).reshape(L, 128, 512)
    cols = np.zeros((128, L, NCOL), f)
    cols[:, :, C_GQ] = np.tile(np.asarray(inputs["sb_q_g"], f)[:L], (1, 2)).T
    cols[:, :, C_GK] = np.tile(np.asarray(inputs["sb_k_g"], f)[:L], (1, 2)).T

    def pc(a):
        return np.asarray(a, f)[:L].reshape(L, 2, 128).transpose(2, 0, 1)
    cols[:, :, C_PS:C_PS + 2] = pc(inputs["pool_scale"])
    cols[:, :, C_CB:C_CB + 2] = pc(inputs["conv_b"])
    cols[:, :, C_LG:C_LG + 2] = pc(inputs["conv_ln_g"])
    cols[:, :, C_LB:C_LB + 2] = pc(inputs["conv_ln_b"])
    cw = np.asarray(inputs["conv_w"], f)[:L]
    cols[:, :, C_CW:C_CW + 2 * CONVK] = cw.reshape(L, CONVK, 2, 128).transpose(3, 0, 2, 1).reshape(128, L, 2 * CONVK)
    gains = np.ascontiguousarray(np.stack([np.asarray(inputs["norm_mix_g"], f)[:L],
                                           np.asarray(inputs["norm_ffn_g"], f)[:L]], axis=1))
    cbf = np.zeros((128, K_END), f)
    idx = np.arange(128)
    cbf[:, K_ID:K_ID + 128] = np.eye(128, dtype=f)
    cbf[:, K_TRI:K_TRI + 128] = (idx[:, None] >= idx[None, :]).astype(f) * -1.0
    cbf[:, K_ONE:K_ONE + 128] = -1.0
    cbf[:, K_NEG:K_NEG + 128] = np.where(idx[:, None] >= idx[None, :], -30000.0, 0.0)
    cbf[:, K_B64:K_B64 + 128] = ((idx[:, None] // 64) == (idx[None, :] // 64)).astype(f) / 64.0
    cf = np.zeros((128, KF_END), f)
    cf[:, KF_B64:KF_B64 + 128] = ((idx[:, None] // 64) == (idx[None, :] // 64)).astype(f) / 64.0
    cf[:, KF_O256:KF_O256 + 128] = 1.0 / 256.0
    for fc in range(2):
        for half in range(2):
            wdw = 2 << (fc * 2 + half)
            t = np.arange(16)
            cf[half * 64:(half + 1) * 64, KF_FIX + fc * 16:KF_FIX + fc * 16 + 16] = wdw / np.minimum(t + 1.0, wdw)
    return dict(win=win, wout=wout, wgu=wgu, wdn=wdn, smw=smw, cols=cols, gains=gains, cbf=cbf, cf=cf)


_CACHE = {}


def kernel(**inputs):
    x = np.asarray(inputs["x"], np.float32)
    shared = host_prep(inputs)
    if 'nc' not in _CACHE:
        _CACHE['nc'] = build_program()[0]
    nc = _CACHE['nc']
    in_maps = []
    for c in range(NCORES):
        m = dict(shared)
        m["x"] = np.ascontiguousarray(x[c * NSEQ_CORE:(c + 1) * NSEQ_CORE])
        in_maps.append(m)
    res = run_bass_kernel_spmd(nc, in_maps, core_ids=list(range(NCORES)))
    return np.concatenate([r["out"] for r in res.results], axis=0).astype(np.float32)
```
